# Optimizing a Trainium2 kernel written in Bass

```python
import math
import jax, jax.numpy as jnp
from jax import lax
import numpy as np

D_MODEL = 1024
BATCH = 4
SEQ = 8192
DEPTH = 2

D_MIX = D_MODEL
HEAD_DIM = 64
D_SC = D_MIX // 4
D_ATT = D_MIX // 2
D_CC = D_MIX // 4
N_Q_HEADS = D_ATT // HEAD_DIM
N_KV_HEADS = 2
GQA_GROUP = N_Q_HEADS // N_KV_HEADS
SC_WIDTH = 3
CC_WIDTH = 31
WINDOW = 128
BLOCK = 128
ROPE_THETA = 10000.0
D_FF = 2816
LN_EPS = 1e-5
ALPHA = (2.0 * DEPTH) ** 0.25
BETA = (8.0 * DEPTH) ** -0.25
IN_SIZES = (D_SC, D_SC, D_SC, N_Q_HEADS * HEAD_DIM, N_KV_HEADS * HEAD_DIM, N_KV_HEADS * HEAD_DIM, D_CC, D_CC)
D_IN = sum(IN_SIZES)
IN_OFFSETS = tuple(int(v) for v in np.cumsum(IN_SIZES)[:-1])

kernel_name = "hybrid_parallel_conv_swa_conformer_encoder"


def layer_norm(x, g, b):
    xf = x.astype(jnp.float32)
    mu = jnp.mean(xf, axis=-1, keepdims=True)
    var = jnp.mean(jnp.square(xf - mu), axis=-1, keepdims=True)
    y = (xf - mu) * lax.rsqrt(var + LN_EPS)
    return (y * g.astype(jnp.float32) + b.astype(jnp.float32)).astype(x.dtype)


def swiglu(x, w_gu, w_down):
    g, u = jnp.split(x @ w_gu, 2, axis=-1)
    return (jax.nn.silu(g) * u) @ w_down


def depthwise_conv(x, w):
    k = w.shape[0]
    pad = (k - 1) // 2
    return lax.conv_general_dilated(
        x, w[:, None, :].astype(x.dtype), window_strides=(1,), padding=[(pad, pad)],
        dimension_numbers=("NWC", "WIO", "NWC"), feature_group_count=x.shape[-1])


def rope(x, positions):
    half = HEAD_DIM // 2
    inv_freq = ROPE_THETA ** (-jnp.arange(half, dtype=jnp.float32) / half)
    ang = positions.astype(jnp.float32)[:, None] * inv_freq[None, :]
    cos = jnp.cos(ang)[None, :, None, :]
    sin = jnp.sin(ang)[None, :, None, :]
    xf = x.astype(jnp.float32)
    x1, x2 = xf[..., :half], xf[..., half:]
    return jnp.concatenate([x1 * cos - x2 * sin, x2 * cos + x1 * sin], axis=-1).astype(x.dtype)


def banded_window_attention(q, k, v, sink):
    b, s = q.shape[0], q.shape[1]
    nb = s // BLOCK
    qb = q.reshape(b, nb, BLOCK, N_KV_HEADS, GQA_GROUP, HEAD_DIM)

    def band(t):
        tp = jnp.pad(t, ((0, 0), (BLOCK, BLOCK), (0, 0), (0, 0)))
        parts = [tp[:, o * BLOCK:o * BLOCK + s].reshape(b, nb, BLOCK, N_KV_HEADS, HEAD_DIM) for o in range(3)]
        return jnp.concatenate(parts, axis=2)

    kb, vb = band(k), band(v)
    scores = jnp.einsum("bnqkgd,bnskd->bnkgqs", qb, kb).astype(jnp.float32) * (HEAD_DIM ** -0.5)
    qpos = jnp.arange(s).reshape(nb, BLOCK)
    kpos = jnp.arange(nb)[:, None] * BLOCK - BLOCK + jnp.arange(3 * BLOCK)[None, :]
    valid = (jnp.abs(qpos[:, :, None] - kpos[:, None, :]) <= WINDOW) \
        & (kpos >= 0)[:, None, :] & (kpos < s)[:, None, :]
    scores = jnp.where(valid[None, :, None, None], scores, -1e30)
    sink_f = sink.astype(jnp.float32).reshape(N_KV_HEADS, GQA_GROUP)[None, None, :, :, None]
    m = jnp.maximum(jnp.max(scores, axis=-1), sink_f)
    p = jnp.exp(scores - m[..., None])
    denom = jnp.sum(p, axis=-1) + jnp.exp(sink_f - m)
    o = jnp.einsum("bnkgqs,bnskd->bnqkgd", p.astype(v.dtype), vb).astype(jnp.float32)
    o = o / jnp.transpose(denom, (0, 1, 4, 2, 3))[..., None]
    return o.reshape(b, s, D_ATT).astype(q.dtype)


def hybrid_mixer(x, w_in, sc_conv_w, attn_sink, cc_conv_w, cc_conv_b, cc_ln_g, cc_ln_b, w_out):
    b, s, _ = x.shape
    positions = jnp.arange(s)
    z = x @ w_in
    sc_b, sc_c, sc_h, q, k, v, cc_a, cc_gate = jnp.split(z, IN_OFFSETS, axis=-1)
    y_sc = sc_b * depthwise_conv(sc_c * sc_h, sc_conv_w)
    q = rope(q.reshape(b, s, N_Q_HEADS, HEAD_DIM), positions)
    k = rope(k.reshape(b, s, N_KV_HEADS, HEAD_DIM), positions)
    v = v.reshape(b, s, N_KV_HEADS, HEAD_DIM)
    y_att = banded_window_attention(q, k, v, attn_sink)
    u = cc_a * jax.nn.sigmoid(cc_gate)
    u = depthwise_conv(u, cc_conv_w) + cc_conv_b
    y_cc = jax.nn.silu(layer_norm(u, cc_ln_g, cc_ln_b))
    return jnp.concatenate([y_sc, y_att, y_cc], axis=-1) @ w_out


def setup_inputs(seed: int = 0) -> dict:
    key = jax.random.key(seed)
    ks = jax.random.split(key, 24)
    f32 = jnp.float32

    def nrm(k, shape, scale):
        return jax.random.normal(k, shape, f32) * scale

    def gain(k, n):
        return 1.0 + 0.02 * jax.random.normal(k, (DEPTH, n), f32)

    return {
        "x": jax.random.normal(ks[0], (BATCH, SEQ, D_MODEL), f32),
        "ffn1_w_gu": nrm(ks[1], (DEPTH, D_MODEL, 2 * D_FF), D_MODEL ** -0.5),
        "ffn1_w_down": nrm(ks[2], (DEPTH, D_FF, D_MODEL), BETA * D_FF ** -0.5),
        "ln1_g": gain(ks[3], D_MODEL),
        "ln1_b": nrm(ks[4], (DEPTH, D_MODEL), 0.02),
        "w_in": nrm(ks[5], (DEPTH, D_MODEL, D_IN), D_MODEL ** -0.5),
        "sc_conv_w": nrm(ks[6], (DEPTH, SC_WIDTH, D_SC), SC_WIDTH ** -0.5),
        "attn_sink": nrm(ks[7], (DEPTH, N_Q_HEADS), 0.5),
        "cc_conv_w": nrm(ks[8], (DEPTH, CC_WIDTH, D_CC), CC_WIDTH ** -0.5),
        "cc_conv_b": nrm(ks[9], (DEPTH, D_CC), 0.02),
        "cc_ln_g": gain(ks[10], D_CC),
        "cc_ln_b": nrm(ks[11], (DEPTH, D_CC), 0.02),
        "w_out": nrm(ks[12], (DEPTH, D_MIX, D_MODEL), BETA * D_MIX ** -0.5),
        "ln2_g": gain(ks[13], D_MODEL),
        "ln2_b": nrm(ks[14], (DEPTH, D_MODEL), 0.02),
        "ffn2_w_gu": nrm(ks[15], (DEPTH, D_MODEL, 2 * D_FF), D_MODEL ** -0.5),
        "ffn2_w_down": nrm(ks[16], (DEPTH, D_FF, D_MODEL), BETA * D_FF ** -0.5),
        "ln3_g": gain(ks[17], D_MODEL),
        "ln3_b": nrm(ks[18], (DEPTH, D_MODEL), 0.02),
    }


def reference(x, ffn1_w_gu, ffn1_w_down, ln1_g, ln1_b, w_in, sc_conv_w, attn_sink, cc_conv_w,
              cc_conv_b, cc_ln_g, cc_ln_b, w_out, ln2_g, ln2_b, ffn2_w_gu, ffn2_w_down, ln3_g, ln3_b):
    for l in range(DEPTH):
        x = layer_norm(ALPHA * x + 0.5 * swiglu(x, ffn1_w_gu[l], ffn1_w_down[l]), ln1_g[l], ln1_b[l])
        x = layer_norm(ALPHA * x + hybrid_mixer(x, w_in[l], sc_conv_w[l], attn_sink[l], cc_conv_w[l],
                                                 cc_conv_b[l], cc_ln_g[l], cc_ln_b[l], w_out[l]),
                       ln2_g[l], ln2_b[l])
        x = layer_norm(ALPHA * x + 0.5 * swiglu(x, ffn2_w_gu[l], ffn2_w_down[l]), ln3_g[l], ln3_b[l])
    return x
```

```python
import os
import numpy as np
import ml_dtypes
import concourse.bass as bass
import concourse.mybir as mybir
from concourse.bass_utils import run_bass_kernel_spmd

F32 = mybir.dt.float32
BF16 = mybir.dt.bfloat16
I32 = mybir.dt.int32
AF = mybir.ActivationFunctionType
ALU = mybir.AluOpType

D = 1024
DFF = 2816
NJ = 22
KC = 8
DEPTH = 2
ALPHA = (2.0 * DEPTH) ** 0.25
EPS = 1e-5
NWIN = 17
HM = [0, 2, 1, 3]
NPPL = 78
ROPE_THETA = 10000.0


class Buf:
    __slots__ = ("w", "r", "rng", "const")

    def __init__(self, rng=None):
        self.w = None
        self.r = []
        self.rng = rng
        self.const = False


class DSem:
    def __init__(self, sem):
        self.sem = sem
        self.count = 0
        self.last = None


class Op:
    __slots__ = ("eng", "fn", "deps", "dsem", "dcount", "sig", "sigidx")


class Sched:
    ENGS = ("pe", "act", "dve", "pool", "sp")

    def __init__(self):
        self.q = {e: [] for e in self.ENGS}
        self.live = []
        self.dsems = []

    def newbufs(self, n, rng=None):
        bs = [Buf(rng) for _ in range(n)]
        if rng is not None:
            inh = []
            keep = []
            for o in self.live:
                if o.rng[0] < rng[1] and rng[0] < o.rng[1]:
                    if o.w is not None:
                        inh.append(o.w)
                    inh.extend(o.r)
                    if not (rng[0] <= o.rng[0] and o.rng[1] <= rng[1]):
                        keep.append(o)
                else:
                    keep.append(o)
            inh = list(dict.fromkeys(inh))
            for b in bs:
                b.r = list(inh)
            keep.extend(bs)
            self.live = keep
        return bs

    def dsem(self, sem):
        d = DSem(sem)
        self.dsems.append(d)
        return d

    def op(self, eng, fn, reads=(), writes=(), dsem=None):
        o = Op()
        o.eng = eng
        o.fn = fn
        o.dsem = dsem
        o.dcount = 0
        o.sig = False
        o.sigidx = 0
        deps = {}
        for b in reads:
            if b.w is not None:
                deps[id(b.w)] = b.w
        for b in writes:
            if b.w is not None:
                deps[id(b.w)] = b.w
            for r in b.r:
                deps[id(r)] = r
        if dsem is not None and dsem.last is not None:
            deps[id(dsem.last)] = dsem.last
        o.deps = [d for d in deps.values()
                  if not (d.dsem is None and d.eng == "pe" and eng == "pe") and d is not o]
        for b in reads:
            if not b.const:
                b.r.append(o)
        for b in writes:
            b.w = o
            b.r = []
        if dsem is not None:
            dsem.count += 16
            o.dcount = dsem.count
            dsem.last = o
        self.q[eng].append(o)
        return o

    def finalize(self):
        for e in self.ENGS:
            for o in self.q[e]:
                for d in o.deps:
                    if d.dsem is None:
                        d.sig = True
        for e in self.ENGS:
            c = 0
            for o in self.q[e]:
                if o.dsem is None and o.sig:
                    c += 1
                    o.sigidx = c

    def emit(self, ename, eng, esem, final_wait=False):
        waited = {}
        for o in self.q[ename]:
            need = {}
            for d in o.deps:
                if d.dsem is not None:
                    key = ("d", id(d.dsem))
                    sem = d.dsem.sem
                    val = d.dcount
                else:
                    key = ("e", d.eng)
                    sem = esem[d.eng]
                    val = d.sigidx
                if waited.get(key, 0) >= val:
                    continue
                if key not in need or need[key][1] < val:
                    need[key] = (sem, val)
            for key, (sem, val) in need.items():
                eng.wait_ge(sem, val)
                waited[key] = val
            ins = o.fn(eng)
            if o.dsem is not None:
                ins.then_inc(o.dsem.sem, 16)
            elif o.sig:
                ins.then_inc(esem[ename], 1)
        if final_wait:
            for d in self.dsems:
                if d.count > 0:
                    eng.wait_ge(d.sem, d.count)


class Ctx:
    pass


class Lay:
    def __init__(self, K):
        self.K = K
        self.off = 0

    def alloc(self, shape, dt, nb=1):
        K = self.K
        esz = 4 if dt == F32 else 2
        n = 1
        for s in shape:
            n *= s
        nbytes = (n * esz + 3) // 4 * 4
        o0 = self.off
        self.off += nbytes
        assert self.off <= K.arena_bytes, (self.off, K.arena_bytes)
        a = K.arena[:, o0 // 4:(o0 + nbytes) // 4]
        if dt != F32:
            a = a.bitcast(dt)
        a = a[:, :n]
        if len(shape) == 2:
            a = a.rearrange("p (a b) -> p a b", a=shape[0])
        elif len(shape) == 3:
            a = a.rearrange("p (a b c) -> p a b c", a=shape[0], b=shape[1])
        bufs = K.S.newbufs(nb, (o0, o0 + nbytes))
        return a, bufs


def split_tiles(xs, mx=4):
    n = len(xs)
    k = (n + mx - 1) // mx
    base, rem = divmod(n, k)
    out, i = [], 0
    for t in range(k):
        sz = base + (1 if t < rem else 0)
        out.append(xs[i:i + sz])
        i += sz
    return out


def plan_half(o0, o1):
    a1, b1 = max(o0 - 1, 0), o1 + 1
    a0, b0 = max(a1 - 1, 0), b1 + 1
    return dict(o=(o0, o1), r1=(a1, b1), r0=(a0, b0))


def emit_transposes(K, xi, yT_ap, yT_buf, pos, bp):
    S = K.S
    for kc in range(KC):
        bank = bp + kc // 4
        col = (kc % 4) * 128
        S.op("pe",
             (lambda e, bank=bank, col=col, kc=kc: e.transpose(
                 out=K.ps[bank][:, col:col + 128], in_=K.X[xi][:, kc * 128:(kc + 1) * 128],
                 identity=K.identf[:])),
             reads=[K.Xb[xi], K.cb], writes=[K.psb[bank]])
    src = K.psall[:, bp * 512:(bp + 2) * 512].rearrange("p (k n) -> p k n", k=KC)
    dst = yT_ap[:, :, pos * 128:(pos + 1) * 128]
    S.op("act", (lambda e: e.activation(out=dst, in_=src, func=AF.Copy)),
         reads=[K.psb[bp], K.psb[bp + 1]], writes=[yT_buf])


def dve_rsqrt(K, v, y, t, n, rbufs, wbuf, tbuf=None, iters=3):
    S = K.S
    yi = y.bitcast(I32)
    vi = v.bitcast(I32)
    S.op("dve", (lambda e: e.tensor_scalar(out=yi[:, :n], in0=vi[:, :n], scalar1=-0.5, scalar2=float(0x5f3759df),
                                           op0=ALU.mult, op1=ALU.add)),
         reads=list(rbufs), writes=[wbuf])
    wt = [wbuf] if tbuf is None else [wbuf, tbuf]
    for _ in range(iters):
        S.op("dve", (lambda e: e.tensor_tensor(out=t[:, :n], in0=y[:, :n], in1=y[:, :n], op=ALU.mult)),
             reads=[wbuf], writes=wt)
        S.op("dve", (lambda e: e.tensor_tensor(out=t[:, :n], in0=t[:, :n], in1=v[:, :n], op=ALU.mult)),
             reads=wt + list(rbufs), writes=wt)
        S.op("dve", (lambda e: e.tensor_scalar(out=t[:, :n], in0=t[:, :n], scalar1=-0.5, scalar2=1.5,
                                               op0=ALU.mult, op1=ALU.add)),
             reads=wt, writes=wt)
        S.op("dve", (lambda e: e.tensor_tensor(out=y[:, :n], in0=y[:, :n], in1=t[:, :n], op=ALU.mult)),
             reads=wt, writes=wt)


def ln_part1(K, xi, bp, lnt, slot, c, rs_=2.0):
    S = K.S
    X = K.X[xi]
    Xb = K.Xb[xi]
    st, mv, ve, y, t, lb = lnt[slot]
    for n in range(2):
        S.op("dve",
             (lambda e, n=n: e.scalar_tensor_tensor(
                 out=X[:, n * 512:(n + 1) * 512], in0=X[:, n * 512:(n + 1) * 512], scalar=rs_ * ALPHA,
                 in1=K.ps[bp + n][:, :], op0=ALU.mult, op1=ALU.add)),
             reads=[Xb, K.psb[bp + n]], writes=[Xb])
    for n in range(2):
        S.op("dve", (lambda e, n=n: e.bn_stats(out=st[:, c, n, :], in_=X[:, n * 512:(n + 1) * 512])),
             reads=[Xb], writes=[lb])
    S.op("dve", (lambda e: e.bn_aggr(out=mv[:, c, :], in_=st[:, c, :, :].rearrange("p a b -> p (a b)"))),
         reads=[lb], writes=[lb])


def ln_part2(K, xis, gbc, bbc, gbbufs, lnt, slot, rs_=2.0, store_gbs=None):
    S = K.S
    st, mv, ve, y, t, lb = lnt[slot]
    nb = len(xis)
    S.op("dve", (lambda e: e.tensor_scalar(out=ve[:, :nb], in0=mv[:, :nb, 1], scalar1=rs_ * rs_ * EPS, scalar2=None,
                                           op0=ALU.add)),
         reads=[lb], writes=[lb])
    dve_rsqrt(K, ve, y, t, nb, [lb], lb)
    for c, xi in enumerate(xis):
        X = K.X[xi]
        Xb = K.Xb[xi]
        S.op("dve", (lambda e, X=X, c=c: e.scalar_tensor_tensor(out=X[:, :], in0=X[:, :], scalar=mv[:, c, 0:1],
                                                                in1=gbc[:, :], op0=ALU.subtract, op1=ALU.mult)),
             reads=[Xb, lb, gbbufs[0]], writes=[Xb])
        S.op("dve", (lambda e, X=X, c=c: e.scalar_tensor_tensor(out=X[:, :], in0=X[:, :], scalar=y[:, c:c + 1],
                                                                in1=bbc[:, :], op0=ALU.mult, op1=ALU.add)),
             reads=[Xb, lb, gbbufs[1]], writes=[Xb])
        if store_gbs is not None:
            ds = K.out_sems[K.out_rr % len(K.out_sems)]
            K.out_rr += 1
            orow = store_gbs[c] * 128
            S.op("sp", (lambda e, X=X, orow=orow: e.dma_start(out=K.out_d[orow:orow + 128, :], in_=X[:, :])),
                 reads=[Xb], dsem=ds)


def alloc_ln(K, lay):
    lnt = []
    for s in range(2):
        st, b1 = lay.alloc([4, 2, 6], F32)
        mv, b2 = lay.alloc([4, 2], F32)
        ve, b3 = lay.alloc([4], F32)
        y, b4 = lay.alloc([4], F32)
        t, b5 = lay.alloc([4], F32)
        lnt.append((st, mv, ve, y, t, b1[0]))
    return lnt


def alloc_gb(K, lay):
    gbc, gb0 = lay.alloc([D], F32)
    bbc, gb1 = lay.alloc([D], F32)
    return gbc, bbc, [gb0[0], gb1[0]]


def emit_gb_load(K, gbc, bbc, gbb, l, which):
    S = K.S
    S.op("pool", (lambda e: e.dma_start(out=gbc[:, :], in_=K.lng_d[l, which, :].partition_broadcast(128))),
         writes=[gbb[0]], dsem=K.misc_sems[0])
    S.op("pool", (lambda e: e.dma_start(out=bbc[:, :], in_=K.lnb_d[l, which, :].partition_broadcast(128))),
         writes=[gbb[1]], dsem=K.misc_sems[1])


def ffn_step(K, l, which, xis, load_x, store_out):
    S = K.S
    wgu = K.wgu_d[which][l]
    wdn = K.wdn_d[which][l]
    lay = Lay(K)
    wg = [lay.alloc([KC, 2, 128], BF16, nb=2) for _ in range(4)]
    gbc, bbc, gbb = alloc_gb(K, lay)
    yT = [lay.alloc([KC, 512], BF16, nb=4) for _ in range(2)]
    h, hb = lay.alloc([NJ, 512], BF16, nb=NJ)
    wdr, wdb = lay.alloc([NJ, D], BF16, nb=NJ)
    sg = [lay.alloc([512], F32) for _ in range(2)]
    lnt = alloc_ln(K, lay)

    def emit_xloads(tb_):
        for xi in tb_:
            gb = K.xbase + xi
            ds = K.x_sems[xi % len(K.x_sems)]
            S.op("pool", (lambda e, xi=xi, gb=gb: e.dma_start(out=K.X[xi][:, :], in_=K.x_d[gb * 128:(gb + 1) * 128, :])),
                 writes=[K.Xb[xi]], dsem=ds)

    tiles = split_tiles(xis)
    if load_x:
        for tb_ in tiles[:2]:
            emit_xloads(tb_)
    wsrc = wgu.rearrange("(kc p) (two c) -> p kc two c", p=128, two=2)

    lq = [0]
    slot_of = {}

    def store_scr(key, dst, src_ap, src_bufs):
        sb = S.newbufs(1)[0]
        K.scr[key] = sb
        ds = K.st_sems[K.st_rr % len(K.st_sems)]
        K.st_rr += 1
        S.op("sp", (lambda e: e.dma_start(out=dst, in_=src_ap)), reads=src_bufs, writes=[sb], dsem=ds)

    def load_wg(j):
        slot = lq[0] % 4
        slot_of[j] = slot
        lq[0] += 1
        key = ("gu", which, l, j)
        flat = wg[slot][0].rearrange("p a b c -> p (a b c)")
        if key not in K.scr:
            for two in range(2):
                src = wsrc[:, :, two, j * 128:(j + 1) * 128]
                S.op("pool", (lambda e, two=two, src=src: e.dma_start(out=wg[slot][0][:, :, two, :], in_=src)),
                     writes=[wg[slot][1][two]], dsem=K.wg_sems[slot * 2 + two])
            store_scr(key, K.wgu_s[which][l][j], flat, wg[slot][1])
        else:
            src = K.wgu_s[which][l][j]
            S.op("sp", (lambda e: e.dma_start(out=flat, in_=src)), reads=[K.scr[key]],
                 writes=wg[slot][1], dsem=K.wg_sems[slot * 2])

    def load_wd(j):
        key = ("dn", which, l, j)
        if key not in K.scr:
            S.op("pool", (lambda e: e.dma_start(out=wdr[:, j, :], in_=wdn[j * 128:(j + 1) * 128, :])),
                 writes=[wdb[j]], dsem=K.wd_sems[j % len(K.wd_sems)])
            store_scr(key, K.wdn_s[which][l][j], wdr[:, j, :], [wdb[j]])
        else:
            src = K.wdn_s[which][l][j]
            S.op("pool", (lambda e: e.dma_start(out=wdr[:, j, :], in_=src)), reads=[K.scr[key]],
                 writes=[wdb[j]], dsem=K.wd_sems[j % len(K.wd_sems)])

    for p, xi in enumerate(tiles[0]):
        emit_transposes(K, xi, yT[0][0], yT[0][1][p], p, 4 + 2 * (p % 2))

    gq = 0
    pq = 0
    pending = []
    for ti, tb in enumerate(tiles):
        nb = len(tb)
        N = nb * 128
        ys, ysb = yT[ti % 2]
        nxt = tiles[ti + 1] if ti + 1 < len(tiles) else None
        if load_x and ti + 2 < len(tiles):
            emit_xloads(tiles[ti + 2])
        if ti == 0:
            load_wg(0)
            load_wg(1)
            load_wg(2)
        for j in range(NJ):
            slot = slot_of.pop(j)
            if j + 3 < NJ:
                load_wg(j + 3)
            elif nxt is not None:
                load_wg(j + 3 - NJ)
            if ti == 0:
                load_wd(j)
            bg, bu = ((0, 1), (2, 3), (4, 5))[gq % 3]
            sgt, sgb = sg[gq % 2]
            gq += 1
            for two, bank in ((0, bg), (1, bu)):
                for kc in range(KC):
                    S.op("pe",
                         (lambda e, two=two, bank=bank, kc=kc, slot=slot, N=N, ys=ys: e.matmul(
                             K.ps[bank][:, :N], lhsT=wg[slot][0][:, kc, two, :], rhs=ys[:, kc, :N],
                             start=(kc == 0), stop=(kc == KC - 1))),
                         reads=[wg[slot][1][two]] + ysb[:nb], writes=[K.psb[bank]])
            S.op("act", (lambda e, bg=bg, sgt=sgt, N=N: e.activation(out=sgt[:, :N], in_=K.ps[bg][:, :N], func=AF.Silu)),
                 reads=[K.psb[bg]], writes=[sgb[0]])
            S.op("dve", (lambda e, bu=bu, sgt=sgt, j=j, N=N: e.tensor_tensor(
                out=h[:, j, :N], in0=sgt[:, :N], in1=K.ps[bu][:, :N], op=ALU.mult)),
                 reads=[sgb[0], K.psb[bu]], writes=[hb[j]])
            if j == 2 and pending:
                pending.pop(0)()
            if nxt is not None and j in (3, 7, 11, 15) and not os.environ.get('NO_ILV'):
                p = (j - 3) // 4
                if p < len(nxt):
                    emit_transposes(K, nxt[p], yT[(ti + 1) % 2][0], yT[(ti + 1) % 2][1][p], p, 6)
        if nxt is not None and os.environ.get('NO_ILV'):
            for p in range(len(nxt)):
                emit_transposes(K, nxt[p], yT[(ti + 1) % 2][0], yT[(ti + 1) % 2][1][p], p, 6)
        if ti == 0:
            emit_gb_load(K, gbc, bbc, gbb, l, 0 if which == 0 else 2)
        for bi, xi in enumerate(tb):
            bp = 2 * (pq % 4)
            pq += 1
            for j in range(NJ):
                for n in range(2):
                    S.op("pe",
                         (lambda e, j=j, n=n, bi=bi, bp=bp: e.matmul(
                             K.ps[bp + n][:, :], lhsT=h[:, j, bi * 128:(bi + 1) * 128],
                             rhs=wdr[:, j, n * 512:(n + 1) * 512], start=(j == 0), stop=(j == NJ - 1))),
                         reads=[hb[j], wdb[j]], writes=[K.psb[bp + n]])
            ln_part1(K, xi, bp, lnt, ti % 2, bi)
        pending.append(lambda tb=tb, ti=ti: ln_part2(
            K, tb, gbc, bbc, gbb, lnt, ti % 2,
            store_gbs=([K.xbase + xi - K.out_base for xi in tb] if store_out else None)))
    while pending:
        pending.pop(0)()


def mixer_step(K, l, in_xis, out_xis, true_edge):
    S = K.S
    lay = Lay(K)
    win = [lay.alloc([KC, 128], BF16) for _ in range(4)]
    rope, ropeb = lay.alloc([2, 512], F32)
    tmp = [lay.alloc([512], F32) for _ in range(2)]
    assert lay.off == 16384
    gbc, bbc, gbb = alloc_gb(K, lay)
    tmp += [lay.alloc([512], F32) for _ in range(2)]
    yT, yTb = lay.alloc([KC, 512], BF16, nb=4)
    qrot = [lay.alloc([4, 512], BF16) for _ in range(2)]
    kk = [lay.alloc([2, 768], BF16) for _ in range(2)]
    vv = [lay.alloc([6, 128], BF16) for _ in range(2)]
    uu = [lay.alloc([2, 544], BF16) for _ in range(2)]
    scg = [lay.alloc([2, 544], F32) for _ in range(2)]
    scb = [lay.alloc([2, 512], F32) for _ in range(2)]
    PT = [lay.alloc([512], BF16) for _ in range(4)]
    attT, attb = lay.alloc([4, 512], BF16, nb=4)
    ysc, yscb = lay.alloc([2, 512], BF16)
    ycc, yccb = lay.alloc([2, 512], BF16)
    dgrp = [lay.alloc([8, 128], BF16) for _ in range(3)]
    vc, vcb = lay.alloc([2, 512], F32, nb=2)
    sq, sqb = lay.alloc([2, 512], F32, nb=2)
    dn, dnb = lay.alloc([512], F32)
    rc, rcb = lay.alloc([512], F32)
    oc, ocb = lay.alloc([512], F32)
    wout, woutb = lay.alloc([KC, D], BF16, nb=KC)
    lnt = alloc_ln(K, lay)
    es, esb = lay.alloc([4], F32)

    pp0 = l * NPPL
    scw = lambda c, k: K.pp[:, pp0 + c * 3 + k: pp0 + c * 3 + k + 1]
    ccw = lambda c, k: K.pp[:, pp0 + 6 + c * 31 + k: pp0 + 6 + c * 31 + k + 1]
    ccb = lambda c: K.pp[:, pp0 + 68 + c: pp0 + 69 + c]
    ccg = lambda c: K.pp[:, pp0 + 70 + c: pp0 + 71 + c]
    ccbe = lambda c: K.pp[:, pp0 + 72 + c: pp0 + 73 + c]

    S.op("act", (lambda e: e.activation(out=es[:, :], in_=K.pp[:, pp0 + 74:pp0 + 78], func=AF.Exp)),
         reads=[K.cb], writes=[esb[0]])
    wo_src = K.wout_d[l]

    def store_scr(key, dst, src_ap, src_bufs):
        sb = S.newbufs(1)[0]
        K.scr[key] = sb
        ds = K.st_sems[K.st_rr % len(K.st_sems)]
        K.st_rr += 1
        S.op("sp", (lambda e: e.dma_start(out=dst, in_=src_ap)), reads=src_bufs, writes=[sb], dsem=ds)

    def load_wout_gb():
        for kc in range(KC):
            key = ("wo", l, kc)
            if key not in K.scr:
                S.op("pool", (lambda e, kc=kc: e.dma_start(out=wout[:, kc, :], in_=wo_src[kc * 128:(kc + 1) * 128, :])),
                     writes=[woutb[kc]], dsem=K.wd_sems[kc % len(K.wd_sems)])
                store_scr(key, K.wout_s[l][kc], wout[:, kc, :], [woutb[kc]])
            else:
                S.op("pool", (lambda e, kc=kc: e.dma_start(out=wout[:, kc, :], in_=K.wout_s[l][kc])),
                     reads=[K.scr[key]], writes=[woutb[kc]], dsem=K.wd_sems[kc % len(K.wd_sems)])
        emit_gb_load(K, gbc, bbc, gbb, l, 1)

    wi_src = K.win_d[l].rearrange("(kc p) c -> p kc c", p=128)

    tiles = split_tiles(in_xis)
    nt = len(tiles)
    out_set = set(out_xis)
    first_in = in_xis[0]
    st = dict(wq=0, pq=0, tq=0, ptq=0, dq=0, ab=0, sb=0)

    def load_win(ci):
        slot = st["wq"] % 4
        st["wq"] += 1
        key = ("wi", l, ci)
        flat = win[slot][0].rearrange("p a b -> p (a b)")
        if key not in K.scr:
            src = wi_src[:, :, ci * 128:(ci + 1) * 128]
            S.op("pool", (lambda e: e.dma_start(out=win[slot][0][:, :, :], in_=src)),
                 writes=[win[slot][1][0]], dsem=K.wi_sems[slot])
            store_scr(key, K.win_s[l][ci], flat, [win[slot][1][0]])
        else:
            src = K.win_s[l][ci]
            S.op("sp", (lambda e: e.dma_start(out=flat, in_=src)), reads=[K.scr[key]],
                 writes=[win[slot][1][0]], dsem=K.wi_sems[slot])
        return slot

    def z_chunk(ci, slot, bank, ys_N, nb):
        N = ys_N
        for kc in range(KC):
            S.op("pe", (lambda e, kc=kc: e.matmul(K.ps[bank][:, :N], lhsT=win[slot][0][:, kc, :], rhs=yT[:, kc, :N],
                                                  start=(kc == 0), stop=(kc == KC - 1))),
                 reads=[win[slot][1][0]] + yTb[:nb], writes=[K.psb[bank]])

    def gettmp():
        t = tmp[st["tq"] % 4]
        st["tq"] += 1
        return t

    CH = [("scg", 0, (2, 4)), ("scg", 1, (3, 5)), ("scb", 0, (0,)), ("scb", 1, (1,)),
          ("q", 0, (6,)), ("q", 1, (7,)), ("q", 2, (8,)), ("q", 3, (9,)),
          ("k", 0, (10,)), ("k", 1, (11,)), ("u", 0, (12, 14)), ("u", 1, (13, 15))]
    VCH = 16
    flat = [VCH] + [ci for _, _, cs in CH for ci in cs]

    def phase_A(t):
        tb = tiles[t]
        nb = len(tb)
        N = nb * 128
        buf = t % 2
        pb = (t - 1) % 2
        gb0 = K.xbase + tb[0]
        kk_a, kk_b = kk[buf][0], kk[buf][1][0]
        vv_a, vv_b = vv[buf][0], vv[buf][1][0]
        uu_a, uu_b = uu[buf][0], uu[buf][1][0]
        sg_a, sg_b = scg[buf][0], scg[buf][1][0]
        sb_a, sb_b = scb[buf][0], scb[buf][1][0]
        q_a, q_b = qrot[buf][0], qrot[buf][1][0]
        nbp = len(tiles[t - 1]) if t > 0 else 0
        if t > 0:
            S.op("pool", (lambda e: e.tensor_copy(out=kk_a[:, :, 0:128], in_=kk[pb][0][:, :, nbp * 128:(nbp + 1) * 128])),
                 reads=[kk[pb][1][0]], writes=[kk_b])
            S.op("pool", (lambda e: e.tensor_copy(out=vv_a[:, 0, :], in_=vv[pb][0][:, nbp, :])),
                 reads=[vv[pb][1][0]], writes=[vv_b])
            S.op("pool", (lambda e: e.tensor_copy(out=uu_a[:, :, 0:16], in_=uu[pb][0][:, :, nbp * 128:nbp * 128 + 16])),
                 reads=[uu[pb][1][0]], writes=[uu_b])
            S.op("pool", (lambda e: e.tensor_copy(out=sg_a[:, :, 0:16], in_=scg[pb][0][:, :, nbp * 128:nbp * 128 + 16])),
                 reads=[scg[pb][1][0]], writes=[sg_b])
        else:
            S.op("pool", (lambda e: e.memset(uu_a[:, :, 0:16], 0.0)), writes=[uu_b])
            S.op("pool", (lambda e: e.memset(sg_a[:, :, 0:16], 0.0)), writes=[sg_b])
        S.op("sp", (lambda e: e.dma_start(out=rope[:, :, :N], in_=K.rope_d[:, :, gb0 * 128:gb0 * 128 + N])),
             writes=[ropeb[0]], dsem=K.misc_sems[2])
        slots = {}
        pend = list(flat)
        for _ in range(3):
            ci = pend.pop(0)
            slots[ci] = load_win(ci)

        def nextload():
            if pend:
                ci = pend.pop(0)
                slots[ci] = load_win(ci)

        nextload()
        vslot = slots[VCH]
        for bi in range(nb):
            for kc in range(KC):
                S.op("pe", (lambda e, bi=bi, kc=kc: e.matmul(
                    K.ps[4][:, bi * 128:(bi + 1) * 128], lhsT=yT[:, kc, bi * 128:(bi + 1) * 128],
                    rhs=win[vslot][0][:, kc, :], start=(kc == 0), stop=(kc == KC - 1))),
                     reads=[win[vslot][1][0], yTb[bi]], writes=[K.psb[4]])
        S.op("act", (lambda e: e.activation(
            out=vv_a[:, 1:1 + nb, :], in_=K.ps[4][:, :N].rearrange("p (a b) -> p a b", a=nb), func=AF.Copy)),
             reads=[K.psb[4]], writes=[vv_b])
        if t > 0:
            S.op("pool", (lambda e: e.tensor_copy(out=vv[pb][0][:, nbp + 1, :], in_=vv_a[:, 1, :])),
                 reads=[vv_b], writes=[vv[pb][1][0]])
        for chi, (kind, c, cs) in enumerate(CH):
            if chi == 8:
                yield
            if len(cs) == 2:
                st["ab"] = (st["ab"] + 1) // 2 * 2
                pair = (st["ab"] % 4, st["ab"] % 4 + 1)
                st["ab"] += 2
            else:
                pair = (st["ab"] % 4, None)
                st["ab"] += 1
            for i, ci in enumerate(cs):
                nextload()
                z_chunk(ci, slots[ci], pair[i], N, nb)
            b0 = pair[0]
            b1 = pair[1]
            if kind == "scg":
                t1, t1b = gettmp()
                S.op("act", (lambda e, t1=t1, b0=b0: e.activation(out=t1[:, :N], in_=K.ps[b0][:, :N], func=AF.Copy)),
                     reads=[K.psb[b0]], writes=[t1b[0]])
                S.op("dve", (lambda e, t1=t1, b1=b1, c=c: e.tensor_tensor(
                    out=sg_a[:, c, 16:16 + N], in0=t1[:, :N], in1=K.ps[b1][:, :N], op=ALU.mult)),
                     reads=[t1b[0], K.psb[b1]], writes=[sg_b])
            elif kind == "scb":
                S.op("act", (lambda e, b0=b0, c=c: e.activation(out=sb_a[:, c, :N], in_=K.ps[b0][:, :N], func=AF.Copy)),
                     reads=[K.psb[b0]], writes=[sb_b])
            elif kind in ("q", "k"):
                t1, t1b = gettmp()
                t2, t2b = gettmp()
                S.op("act", (lambda e, t1=t1, b0=b0: e.activation(out=t1[:, :N], in_=K.ps[b0][:, :N], func=AF.Copy)),
                     reads=[K.psb[b0]], writes=[t1b[0]])
                for sblk in range(4):
                    dblk = sblk ^ 1
                    if sblk < 3:
                        S.op("act", (lambda e, t1=t1, t2=t2, sblk=sblk, dblk=dblk: e.activation(
                            out=t2[dblk * 32:(dblk + 1) * 32, :N], in_=t1[sblk * 32:(sblk + 1) * 32, :N],
                            func=AF.Copy)),
                             reads=[t1b[0]], writes=[t2b[0]])
                    else:
                        S.op("dve", (lambda e, t1=t1, t2=t2, sblk=sblk, dblk=dblk: e.tensor_copy(
                            out=t2[dblk * 32:(dblk + 1) * 32, :N], in_=t1[sblk * 32:(sblk + 1) * 32, :N])),
                             reads=[t1b[0]], writes=[t2b[0]])
                S.op("dve", (lambda e, t1=t1: e.tensor_tensor(
                    out=t1[:, :N], in0=t1[:, :N], in1=rope[:, 0, :N], op=ALU.mult)),
                     reads=[t1b[0], ropeb[0]], writes=[t1b[0]])
                S.op("dve", (lambda e, t2=t2: e.tensor_tensor(
                    out=t2[:, :N], in0=t2[:, :N], in1=rope[:, 1, :N], op=ALU.mult)),
                     reads=[t2b[0], ropeb[0]], writes=[t2b[0]])
                if kind == "q":
                    S.op("pool", (lambda e, t1=t1, t2=t2, c=c: e.tensor_tensor(
                        out=q_a[:, c, :N], in0=t1[:, :N], in1=t2[:, :N], op=ALU.add)),
                         reads=[t1b[0], t2b[0]], writes=[q_b])
                else:
                    S.op("pool", (lambda e, t1=t1, t2=t2, c=c: e.tensor_tensor(
                        out=kk_a[:, c, 128:128 + N], in0=t1[:, :N], in1=t2[:, :N], op=ALU.add)),
                         reads=[t1b[0], t2b[0]], writes=[kk_b])
            elif kind == "u":
                t1, t1b = gettmp()
                S.op("act", (lambda e, t1=t1, b1=b1: e.activation(out=t1[:, :N], in_=K.ps[b1][:, :N], func=AF.Sigmoid)),
                     reads=[K.psb[b1]], writes=[t1b[0]])
                S.op("dve", (lambda e, t1=t1, b0=b0, c=c: e.tensor_tensor(
                    out=uu_a[:, c, 16:16 + N], in0=t1[:, :N], in1=K.ps[b0][:, :N], op=ALU.mult)),
                     reads=[t1b[0], K.psb[b0]], writes=[uu_b])
        yield
        if t > 0:
            S.op("pool", (lambda e: e.tensor_copy(out=kk[pb][0][:, :, (nbp + 1) * 128:(nbp + 2) * 128], in_=kk_a[:, :, 128:256])),
                 reads=[kk_b], writes=[kk[pb][1][0]])
            S.op("pool", (lambda e: e.tensor_copy(out=uu[pb][0][:, :, 16 + nbp * 128:32 + nbp * 128], in_=uu_a[:, :, 16:32])),
                 reads=[uu_b], writes=[uu[pb][1][0]])
            S.op("pool", (lambda e: e.tensor_copy(out=scg[pb][0][:, :, 16 + nbp * 128:32 + nbp * 128], in_=sg_a[:, :, 16:32])),
                 reads=[sg_b], writes=[scg[pb][1][0]])

    def transposes_for(t):
        tb = tiles[t]
        for p, xi in enumerate(tb):
            emit_transposes(K, xi, yT, yTb[p], p, 4 + 2 * (p % 2))

    def phase_B1(t, next_tr=None):
        tb = tiles[t]
        buf = t % 2
        kk_a, kk_b = kk[buf][0], kk[buf][1][0]
        vv_a, vv_b = vv[buf][0], vv[buf][1][0]
        uu_a, uu_b = uu[buf][0], uu[buf][1][0]
        sg_a, sg_b = scg[buf][0], scg[buf][1][0]
        sb_a, sb_b = scb[buf][0], scb[buf][1][0]
        q_a, q_b = qrot[buf][0], qrot[buf][1][0]
        outs = [(k, xi) for k, xi in enumerate(tb) if xi in out_set]
        if not outs:
            return
        k0 = outs[0][0]
        nob = len(outs)
        NO = nob * 128
        o0 = k0 * 128
        its = []
        for bidx, (k, xi) in enumerate(outs):
            kbs = []
            if not (xi == first_in):
                kbs.append((k, 0))
            kbs.append((k + 1, None))
            kbs.append((k + 2, 1))
            for g in range(2):
                for ii, (kp, mk) in enumerate(kbs):
                    its.append(dict(k=k, g=g, ii=ii, kp=kp, mk=mk, nk=len(kbs), ob=0,
                                    last=(g == 1 and ii == len(kbs) - 1)))

        def emit_scores(it):
            sa, sbk = ((2, 3), (4, 5), (6, 7))[st["sb"] % 3]
            st["sb"] += 1
            it["sa"], it["sbk"] = sa, sbk
            it["pt"], it["ptb"] = PT[st["ptq"] % 4]
            st["ptq"] += 1
            k, g, kp, mk = it["k"], it["g"], it["kp"], it["mk"]
            if mk is not None:
                for hi, bank in ((0, sa), (1, sbk)):
                    S.op("pe", (lambda e, bank=bank, mk=mk: e.matmul(
                        K.ps[bank][:, 0:256], lhsT=K.identb[:, :], rhs=K.maskneg[:, mk, :],
                        start=True, stop=False)),
                         reads=[K.cb], writes=[K.psb[bank]])
            for hi, bank in ((0, sa), (1, sbk)):
                S.op("pe", (lambda e, hi=hi, bank=bank, kp=kp, g=g, k=k, mk=mk: e.matmul(
                    K.ps[bank][:, 0:256].rearrange("p (a b) -> p a b", a=2),
                    lhsT=kk_a[hi * 64:(hi + 1) * 64, g, kp * 128:(kp + 1) * 128],
                    rhs=q_a[hi * 64:(hi + 1) * 64, 2 * g:2 * g + 2, k * 128:(k + 1) * 128],
                    start=(mk is None), stop=True)),
                     reads=[kk_b, q_b], writes=[K.psb[bank]])

        def emit_soft(it):
            pt, ptb = it["pt"], it["ptb"]
            for hi, bank in ((0, it["sa"]), (1, it["sbk"])):
                S.op("act", (lambda e, hi=hi, bank=bank, pt=pt: e.activation(
                    out=pt[:, hi * 256:(hi + 1) * 256], in_=K.ps[bank][:, 0:256], func=AF.Exp, scale=0.125)),
                     reads=[K.psb[bank]], writes=[ptb[0]])

        def emit_pv(it):
            pt, ptb, kp, g, ii, nk, ob, k = it["pt"], it["ptb"], it["kp"], it["g"], it["ii"], it["nk"], it["ob"], it["k"]
            S.op("pe", (lambda e: e.matmul(
                K.ps[ob][g * 64:(g + 1) * 64, :], lhsT=vv_a[:, kp, g * 64:(g + 1) * 64], rhs=pt[:, :],
                start=(ii == 0), stop=(ii == nk - 1))),
                 reads=[vv_b, ptb[0]], writes=[K.psb[ob]])
            S.op("pe", (lambda e: e.matmul(
                K.ps[ob + 1][g * 64:(g + 1) * 64, :], lhsT=K.onesb[:, 0:64], rhs=pt[:, :],
                start=(ii == 0), stop=(ii == nk - 1))),
                 reads=[K.cb, ptb[0]], writes=[K.psb[ob + 1]])
            if it["last"]:
                S.op("act", (lambda e: e.activation(out=oc[:, :], in_=K.ps[ob][:, :], func=AF.Copy)),
                     reads=[K.psb[ob]], writes=[ocb[0]])
                S.op("act", (lambda e: e.activation(out=dn[:, :], in_=K.ps[ob + 1][:, :], func=AF.Copy)),
                     reads=[K.psb[ob + 1]], writes=[dnb[0]])
                for e4 in range(4):
                    S.op("dve", (lambda e, e4=e4: e.tensor_scalar(
                        out=dn[:, e4 * 128:(e4 + 1) * 128], in0=dn[:, e4 * 128:(e4 + 1) * 128],
                        scalar1=es[:, e4:e4 + 1], scalar2=None, op0=ALU.add)),
                         reads=[esb[0], dnb[0]], writes=[dnb[0]])
                S.op("dve", (lambda e: e.reciprocal(out=rc[:, :], in_=dn[:, :])), reads=[dnb[0]], writes=[rcb[0]])
                S.op("dve", (lambda e: e.tensor_tensor(
                    out=attT[:, :, k * 128:(k + 1) * 128], in0=oc[:, :].rearrange("p (a b) -> p a b", a=4),
                    in1=rc[:, :].rearrange("p (a b) -> p a b", a=4), op=ALU.mult)),
                     reads=[ocb[0], rcb[0]], writes=[attb[k]])

        for c in range(2):
            a1, a1b = gettmp()
            base = 16 + o0
            S.op("dve", (lambda e, a1=a1, c=c, base=base: e.tensor_scalar(
                out=a1[:, :NO], in0=sg_a[:, c, base - 1:base - 1 + NO], scalar1=scw(c, 0), scalar2=None,
                op0=ALU.mult)),
                 reads=[sg_b, K.cb], writes=[a1b[0]])
            for kt in (1, 2):
                S.op("dve", (lambda e, a1=a1, c=c, base=base, kt=kt: e.scalar_tensor_tensor(
                    out=a1[:, :NO], in0=sg_a[:, c, base - 1 + kt:base - 1 + kt + NO], scalar=scw(c, kt),
                    in1=a1[:, :NO], op0=ALU.mult, op1=ALU.add)),
                     reads=[sg_b, K.cb, a1b[0]], writes=[a1b[0]])
            S.op("dve", (lambda e, a1=a1, c=c: e.tensor_tensor(
                out=ysc[:, c, o0:o0 + NO], in0=a1[:, :NO], in1=sb_a[:, c, o0:o0 + NO], op=ALU.mult)),
                 reads=[a1b[0], sb_b], writes=[yscb[0]])
        while pendB:
            pendB.pop(0)()
        st["sb"] = 0
        for i0 in range(min(2, len(its))):
            emit_scores(its[i0])
        for i, it in enumerate(its):
            if i + 2 < len(its):
                emit_scores(its[i + 2])
            emit_soft(it)
            emit_pv(it)
        if next_tr is not None:
            transposes_for(next_tr)
        for c in range(2):
            bank = 6 + c
            for g0 in range(0, 31, 8):
                ng = min(8, 31 - g0)
                dg, dgb = dgrp[st["dq"] % 3]
                st["dq"] += 1
                col = pp0 + 6 + c * 31 + g0
                S.op("pool", (lambda e, dg=dg, ng=ng, col=col: e.tensor_tensor(
                    out=dg[:, :ng, :], in0=K.identb[:, :].unsqueeze(1).broadcast_to([128, ng, 128]),
                    in1=K.pp[:, col:col + ng].unsqueeze(2).broadcast_to([128, ng, 128]), op=ALU.mult)),
                     reads=[K.cb], writes=[dgb[0]])
                for i in range(ng):
                    kt = g0 + i
                    base = 16 + o0 + kt - 15
                    S.op("pe", (lambda e, dg=dg, i=i, c=c, kt=kt, base=base, bank=bank: e.matmul(
                        K.ps[bank][:, :NO], lhsT=dg[:, i, :], rhs=uu_a[:, c, base:base + NO],
                        start=(kt == 0), stop=(kt == 30))),
                         reads=[dgb[0], uu_b], writes=[K.psb[bank]])
            S.op("act", (lambda e, c=c, bank=bank: e.activation(
                out=vc[:, c, :NO], in_=K.ps[bank][:, :NO], func=AF.Identity, bias=ccb(c), scale=1.0)),
                 reads=[K.psb[bank], K.cb], writes=[vcb[c]])
            S.op("act", (lambda e, c=c: e.activation(out=sq[:, c, :NO], in_=vc[:, c, :NO], func=AF.Square)),
                 reads=[vcb[c]], writes=[sqb[c]])
        for c in range(2):
            S.op("pe", (lambda e, c=c: e.matmul(K.ps[6][:, :NO], lhsT=K.onesf[:, :], rhs=vc[:, c, :NO],
                                                start=(c == 0), stop=(c == 1))),
                 reads=[K.cb, vcb[c]], writes=[K.psb[6]])
        for c in range(2):
            S.op("pe", (lambda e, c=c: e.matmul(K.ps[7][:, :NO], lhsT=K.onesf[:, :], rhs=sq[:, c, :NO],
                                                start=(c == 0), stop=(c == 1))),
                 reads=[K.cb, sqb[c]], writes=[K.psb[7]])
        m2, m2b = sq[:, 0, :], sqb[0]
        rsd, rsdb = sq[:, 1, :], sqb[1]
        mt, mtb = gettmp()
        S.op("act", (lambda e, mt=mt: e.activation(out=mt[:, :NO], in_=K.ps[6][:, :NO], func=AF.Copy)),
             reads=[K.psb[6]], writes=[mtb[0]])
        S.op("dve", (lambda e, mt=mt: e.tensor_tensor(out=m2[:, :NO], in0=mt[:, :NO], in1=mt[:, :NO], op=ALU.mult)),
             reads=[mtb[0]], writes=[m2b])
        S.op("dve", (lambda e: e.scalar_tensor_tensor(out=m2[:, :NO], in0=K.ps[7][:, :NO], scalar=EPS, in1=m2[:, :NO],
                                                      op0=ALU.add, op1=ALU.subtract)),
             reads=[K.psb[7], m2b], writes=[m2b])
        rt, rtb = gettmp()
        dve_rsqrt(K, m2, rsd, rt, NO, [m2b], rsdb, tbuf=rtb[0], iters=2)
        for c in range(2):
            S.op("dve", (lambda e, c=c, mt=mt: e.tensor_tensor(out=vc[:, c, :NO], in0=vc[:, c, :NO], in1=mt[:, :NO],
                                                               op=ALU.subtract)),
                 reads=[vcb[c], mtb[0]], writes=[vcb[c]])
            S.op("dve", (lambda e, c=c: e.tensor_tensor(out=vc[:, c, :NO], in0=vc[:, c, :NO], in1=rsd[:, :NO],
                                                        op=ALU.mult)),
                 reads=[vcb[c], rsdb], writes=[vcb[c]])

    def emit_silu(t):
        tb = tiles[t]
        outs = [(k, xi) for k, xi in enumerate(tb) if xi in out_set]
        if not outs:
            return
        o0 = outs[0][0] * 128
        NO = len(outs) * 128
        for c in range(2):
            S.op("act", (lambda e, c=c: e.activation(out=ycc[:, c, o0:o0 + NO], in_=vc[:, c, :NO], func=AF.Silu,
                                                     bias=ccbe(c), scale=ccg(c))),
                 reads=[vcb[c], K.cb], writes=[yccb[0]])

    def phase_B2(t):
        tb = tiles[t]
        outs = [(k, xi) for k, xi in enumerate(tb) if xi in out_set]
        if not outs:
            return
        k0 = outs[0][0]
        nob = len(outs)
        NO = nob * 128
        o0 = k0 * 128
        for k, xi in outs:
            bp = 4 if st["pq"] % 2 == 0 else 6
            st["pq"] += 1
            for kc in range(KC):
                if kc < 2:
                    src, sbuf = ysc[:, kc, k * 128:(k + 1) * 128], yscb[0]
                elif kc < 6:
                    src, sbuf = attT[:, kc - 2, k * 128:(k + 1) * 128], attb[k]
                else:
                    src, sbuf = ycc[:, kc - 6, k * 128:(k + 1) * 128], yccb[0]
                for n in range(2):
                    S.op("pe", (lambda e, src=src, kc=kc, n=n, bp=bp: e.matmul(
                        K.ps[bp + n][:, :], lhsT=src, rhs=wout[:, kc, n * 512:(n + 1) * 512],
                        start=(kc == 0), stop=(kc == KC - 1))),
                         reads=[sbuf, woutb[kc]], writes=[K.psb[bp + n]])
            ln_part1(K, xi, bp, lnt, t % 2, k - k0, rs_=1.0)
        pendB.append(lambda outs=outs, t=t: ln_part2(K, [xi for _, xi in outs], gbc, bbc, gbb, lnt, t % 2, rs_=1.0))

    pendB = []

    def run_all(g):
        for _ in g:
            pass

    transposes_for(0)
    run_all(phase_A(0))
    if nt > 1:
        transposes_for(1)
        run_all(phase_A(1))
    load_wout_gb()
    for t in range(nt):
        phase_B1(t, next_tr=(t + 2 if t + 2 < nt else None))
        if t + 2 < nt:
            g = phase_A(t + 2)
            next(g)
            emit_silu(t)
            next(g)
            phase_B2(t)
            run_all(g)
        else:
            emit_silu(t)
            phase_B2(t)
    while pendB:
        pendB.pop(0)()


def build(nb_tok, halves=None, dbg_steps=None):
    nb_in = nb_tok + 2
    T_in = nb_in * 128
    if halves is None:
        hh = nb_tok // 2
        halves = [(0, hh), (hh, nb_tok)]
    nx = max(halves[0][1] + 2, nb_tok + 2 - (halves[0][1] - 1))

    nc = bass.Bass("TRN2", target_bir_lowering=False)
    K = Ctx()
    K.nc = nc
    K.S = Sched()
    S = K.S
    dt = lambda name, shape, dtype, kind="ExternalInput": nc.dram_tensor(name, shape, dtype, kind=kind).ap()
    K.x_d = dt("x", [T_in, D], F32)
    K.wgu_d = [dt("wgu1", [DEPTH, D, 2 * DFF], F32), dt("wgu2", [DEPTH, D, 2 * DFF], F32)]
    K.wdn_d = [dt("wdn1", [DEPTH, DFF, D], F32), dt("wdn2", [DEPTH, DFF, D], F32)]
    K.win_d = dt("win", [DEPTH, D, NWIN * 128], F32)
    K.wout_d = dt("wout", [DEPTH, D, D], F32)
    K.lng_d = dt("lng", [DEPTH, 3, D], F32)
    K.lnb_d = dt("lnb", [DEPTH, 3, D], F32)
    pp_d = dt("pp", [128, DEPTH * NPPL], F32)
    K.rope_d = dt("rope", [128, 2, T_in], F32)
    cf_d = dt("cstf", [128, 256], F32)
    cb_d = dt("cstb", [128, 128 + 64 + 512], BF16)
    K.out_d = dt("out", [nb_tok * 128, D], F32, kind="ExternalOutput")
    K.wgu_s = [[dt(f"wgus{w}{l}", [NJ, 128, KC * 2 * 128], BF16, kind="Internal") for l in range(DEPTH)] for w in range(2)]
    K.wdn_s = [[dt(f"wdns{w}{l}", [NJ, 128, D], BF16, kind="Internal") for l in range(DEPTH)] for w in range(2)]
    K.win_s = [dt(f"wins{l}", [NWIN, 128, KC * 128], BF16, kind="Internal") for l in range(DEPTH)]
    K.wout_s = [dt(f"wouts{l}", [KC, 128, D], BF16, kind="Internal") for l in range(DEPTH)]
    K.scr = {}
    K.st_rr = 0
    K.stash_d = [dt(f"stash{i}", [2, 128, D], F32, kind="Internal") for i in range(2)]

    from contextlib import ExitStack
    with ExitStack() as es:
        sb = lambda name, shape, dtype: es.enter_context(nc.sbuf_tensor(name, shape, dtype))
        Xall = sb("Xall", [128, nx, D], F32)
        K.X = [Xall[:, i, :] for i in range(nx)]
        K.Xb = S.newbufs(nx)
        cstf = sb("cstf_sb", [128, 256], F32)
        cstb = sb("cstb_sb", [128, 128 + 64 + 512], BF16)
        K.pp = sb("pp_sb", [128, DEPTH * NPPL], F32)
        K.identf = cstf[:, 0:128]
        K.onesf = cstf[:, 128:256]
        K.identb = cstb[:, 0:128]
        K.onesb = cstb[:, 128:128 + 64]
        K.maskneg = cstb[:, 128 + 64:128 + 64 + 512].rearrange("p (a b) -> p a b", a=2)
        K.cb = S.newbufs(1)[0]
        rem = nc.sbuf_bytes_remaining
        K.arena_bytes = (rem - 64) // 4 * 4
        K.arena = sb("arena", [128, K.arena_bytes // 4], F32)
        K.psall = es.enter_context(nc.psum_tensor("psall", [128, 4096], F32))
        K.ps = [K.psall[:, i * 512:(i + 1) * 512] for i in range(8)]
        K.psb = S.newbufs(8)
        sem = lambda name: es.enter_context(nc.semaphore(name))
        esem = {e: sem("e_" + e) for e in Sched.ENGS}
        K.wg_sems = [S.dsem(sem(f"wg{i}")) for i in range(8)]
        K.st_sems = [S.dsem(sem(f"st{i}")) for i in range(8)]
        K.wi_sems = [S.dsem(sem(f"wi{i}")) for i in range(4)]
        K.wd_sems = [S.dsem(sem(f"wd{i}")) for i in range(6)]
        K.x_sems = [S.dsem(sem(f"xs{i}")) for i in range(4)]
        K.out_sems = [S.dsem(sem(f"os{i}")) for i in range(4)]
        K.misc_sems = [S.dsem(sem(f"ms{i}")) for i in range(4)]
        K.out_rr = 0
        K.out_base = 0

        cbs = S.newbufs(4)
        K.cjoin = sb("cjoin", [128, 2], F32)
        S.op("sp", (lambda e: e.dma_start(out=cstf[:, :], in_=cf_d[:, :])), writes=[cbs[0]], dsem=K.misc_sems[3])
        S.op("sp", (lambda e: e.dma_start(out=cstb[:, :], in_=cb_d[:, :])), writes=[cbs[1]], dsem=K.misc_sems[3])
        S.op("sp", (lambda e: e.dma_start(out=K.pp[:, :], in_=pp_d[:, :])), writes=[cbs[2]], dsem=K.misc_sems[3])
        S.op("pool", (lambda e: e.memset(K.cjoin[:, :], 0.0)), writes=[cbs[3]])
        S.op("pool", (lambda e: e.memset(K.cjoin[:, :], 0.0)), reads=cbs, writes=[K.cb])
        K.cb.const = True

        assert len(halves) == 2 and halves[0][0] == 0 and halves[0][1] == halves[1][0] and halves[1][1] == nb_tok
        h, n = halves[0][1], nb_tok
        stash_b = {1: S.newbufs(2), 2: S.newbufs(2)}
        blk_of = {1: [h, h + 1], 2: [h - 1, h]}

        def rng(a, b):
            return list(range(a - K.xbase, b - K.xbase))

        def stash(which, store):
            for i, blk in enumerate(blk_of[which]):
                xi = blk - K.xbase
                ds = K.x_sems[i % len(K.x_sems)]
                if store:
                    S.op("sp", (lambda e, xi=xi, i=i: e.dma_start(out=K.stash_d[which - 1][i], in_=K.X[xi][:, :])),
                         reads=[K.Xb[xi]], writes=[stash_b[which][i]], dsem=ds)
                else:
                    S.op("sp", (lambda e, xi=xi, i=i: e.dma_start(out=K.X[xi][:, :], in_=K.stash_d[which - 1][i])),
                         reads=[stash_b[which][i]], writes=[K.Xb[xi]], dsem=ds)

        K.xbase = 0
        ffn_step(K, 0, 0, rng(0, h + 2), load_x=True, store_out=False)
        stash(1, True)
        mixer_step(K, 0, rng(0, h + 2), rng(0, h + 1), True)
        ffn_step(K, 0, 1, rng(0, h + 1), load_x=False, store_out=False)
        ffn_step(K, 1, 0, rng(0, h + 1), load_x=False, store_out=False)
        stash(2, True)
        mixer_step(K, 1, rng(0, h + 1), rng(0, h), True)
        ffn_step(K, 1, 1, rng(0, h), load_x=False, store_out=True)

        K.xbase = h - 1
        ffn_step(K, 0, 0, rng(h + 2, n + 2), load_x=True, store_out=False)
        stash(1, False)
        mixer_step(K, 0, rng(h, n + 2), rng(h + 1, n + 1), False)
        ffn_step(K, 0, 1, rng(h + 1, n + 1), load_x=False, store_out=False)
        ffn_step(K, 1, 0, rng(h + 1, n + 1), load_x=False, store_out=False)
        stash(2, False)
        mixer_step(K, 1, rng(h - 1, n + 1), rng(h, n), False)
        ffn_step(K, 1, 1, rng(h, n), load_x=False, store_out=True)

        S.finalize()
        with nc.Block() as block:
            @block.tensor
            def _(e):
                S.emit("pe", e, esem)

            @block.scalar
            def _(e):
                S.emit("act", e, esem)

            @block.vector
            def _(e):
                S.emit("dve", e, esem)

            @block.gpsimd
            def _(e):
                S.emit("pool", e, esem)

            @block.sync
            def _(e):
                S.emit("sp", e, esem, final_wait=True)
    return nc


def _win_cols():
    cols = []
    cols += list(range(0, 768))
    cols += list(range(768, 1280))
    for g in range(2):
        cols += [1280 + g * 64 + d for d in range(64)] * 2
    cols += list(range(1536, 2048))
    cols += list(range(1408, 1536))
    return np.asarray(cols)


def _wout_rows():
    rows = list(range(0, 256))
    for e in range(4):
        for g in range(2):
            hd = 4 * g + HM[e]
            rows += [256 + hd * 64 + d for d in range(64)]
    rows += list(range(768, 1024))
    return np.asarray(rows)


def _consts():
    cf = np.zeros((128, 256), np.float32)
    cf[:, 0:128] = np.eye(128, dtype=np.float32)
    cf[:, 128:256] = 1.0 / 256.0
    cb = np.zeros((128, 128 + 64 + 512), np.float32)
    cb[:, 0:128] = np.eye(128)
    j = np.arange(128)[:, None]
    i = np.arange(128)[None, :]
    mprev = (i <= j).astype(np.float32)
    mnext = (j <= i).astype(np.float32)
    cb[:, 128:128 + 64] = 1.0
    cb[:, 128 + 64:128 + 64 + 256] = np.tile((mprev - 1.0) * 30000.0, (1, 2))
    cb[:, 128 + 64 + 256:] = np.tile((mnext - 1.0) * 30000.0, (1, 2))
    return cf, cb.astype(ml_dtypes.bfloat16)


def _rope_table(pos):
    half = 32
    inv_freq = (np.float32(ROPE_THETA) ** (-(np.arange(half, dtype=np.float32) / np.float32(half)))).astype(np.float32)
    ang = pos.astype(np.float32)[:, None] * inv_freq[None, :]
    cos = np.cos(ang).astype(np.float32)
    sin = np.sin(ang).astype(np.float32)
    p = np.arange(128)
    d = p % 64
    jj = d % 32
    sign = np.where(d < 32, -1.0, 1.0).astype(np.float32)
    tab = np.empty((128, 2, pos.shape[0]), np.float32)
    tab[:, 0, :] = cos[:, jj].T
    tab[:, 1, :] = sin[:, jj].T * sign[:, None]
    return tab


def _pp(sc_w, cc_w, cc_b, cc_g, cc_be, sink, flip):
    pp = np.zeros((128, DEPTH * NPPL), np.float32)
    p = np.arange(128)
    for l in range(DEPTH):
        scw = sc_w[l][::-1] if flip else sc_w[l]
        ccw = cc_w[l][::-1] if flip else cc_w[l]
        o = l * NPPL
        for c in range(2):
            for k in range(3):
                pp[:, o + c * 3 + k] = scw[k, c * 128 + p]
            for k in range(31):
                pp[:, o + 6 + c * 31 + k] = ccw[k, c * 128 + p]
            pp[:, o + 68 + c] = cc_b[l, c * 128 + p]
            pp[:, o + 70 + c] = cc_g[l, c * 128 + p]
            pp[:, o + 72 + c] = cc_be[l, c * 128 + p]
        for e in range(4):
            pp[:, o + 74 + e] = sink[l, 4 * (p // 64) + HM[e]]
    return pp


_NC_CACHE = {}


def run(inputs, dbg_steps=None, halves=None):
    x = np.asarray(inputs["x"], np.float32)
    B, SEQ, _ = x.shape
    TOK = SEQ // 2
    nb_tok = TOK // 128
    T_in = TOK + 256
    ncores = 2 * B
    key = (nb_tok, dbg_steps, str(halves))
    if key not in _NC_CACHE:
        _NC_CACHE[key] = build(nb_tok, halves=halves, dbg_steps=dbg_steps)
    nc = _NC_CACHE[key]
    f = lambda k: np.ascontiguousarray(np.asarray(inputs[k], np.float32))
    cols = _win_cols()
    rows = _wout_rows()
    win = np.ascontiguousarray(f("w_in")[:, :, cols])
    wout = np.ascontiguousarray(f("w_out")[:, rows, :])
    lng = np.ascontiguousarray(np.stack([f("ln1_g"), f("ln2_g"), f("ln3_g")], axis=1))
    lnb = np.ascontiguousarray(np.stack([f("ln1_b"), f("ln2_b"), f("ln3_b")], axis=1))
    cf, cb = _consts()
    shared = dict(wgu1=f("ffn1_w_gu"), wgu2=f("ffn2_w_gu"), wdn1=f("ffn1_w_down"), wdn2=f("ffn2_w_down"),
                  win=win, wout=wout, lng=lng, lnb=lnb, cstf=cf, cstb=cb)
    in_maps = []
    for c in range(ncores):
        b, hf = c // 2, c % 2
        if hf == 0:
            xs = x[b, 0:T_in]
            pos = np.arange(T_in)
        else:
            xs = x[b, SEQ - T_in:SEQ][::-1]
            pos = SEQ - 1 - np.arange(T_in)
        m = dict(shared)
        m["x"] = np.ascontiguousarray(xs)
        m["rope"] = _rope_table(pos)
        m["pp"] = _pp(f("sc_conv_w"), f("cc_conv_w"), f("cc_conv_b"), f("cc_ln_g"), f("cc_ln_b"), f("attn_sink"),
                      flip=(hf == 1))
        in_maps.append(m)
    res = run_bass_kernel_spmd(nc, in_maps, core_ids=list(range(ncores)))
    out = np.empty((B, SEQ, D), np.float32)
    for c in range(ncores):
        b, hf = c // 2, c % 2
        o = np.asarray(res.results[c]["out"], np.float32)
        if hf == 0:
            out[b, 0:TOK] = o
        else:
            out[b, SEQ - TOK:SEQ] = o[::-1]
    return out


def kernel(**inputs):
    return run(inputs)
```

```python
import os
import numpy as np
import ml_dtypes
import concourse.bass as bass
import concourse.mybir as mybir
from concourse.bass_utils import run_bass_kernel_spmd

F32 = mybir.dt.float32
BF16 = mybir.dt.bfloat16
I32 = mybir.dt.int32
AF = mybir.ActivationFunctionType
ALU = mybir.AluOpType

D = 1024
DFF = 2816
NJ = 22
KC = 8
DEPTH = 2
ALPHA = (2.0 * DEPTH) ** 0.25
EPS = 1e-5
NWIN = 17
HM = [0, 2, 1, 3]
NPPL = 78
ROPE_THETA = 10000.0


class Buf:
    __slots__ = ("w", "r", "rng", "const")

    def __init__(self, rng=None):
        self.w = None
        self.r = []
        self.rng = rng
        self.const = False


class DSem:
    def __init__(self, sem):
        self.sem = sem
        self.count = 0
        self.last = None


class Op:
    __slots__ = ("eng", "fn", "deps", "dsem", "dcount", "sig", "sigidx")


class Sched:
    ENGS = ("pe", "act", "dve", "pool", "sp")

    def __init__(self):
        self.q = {e: [] for e in self.ENGS}
        self.live = []
        self.dsems = []

    def newbufs(self, n, rng=None):
        bs = [Buf(rng) for _ in range(n)]
        if rng is not None:
            inh = []
            keep = []
            for o in self.live:
                if o.rng[0] < rng[1] and rng[0] < o.rng[1]:
                    if o.w is not None:
                        inh.append(o.w)
                    inh.extend(o.r)
                    if not (rng[0] <= o.rng[0] and o.rng[1] <= rng[1]):
                        keep.append(o)
                else:
                    keep.append(o)
            inh = list(dict.fromkeys(inh))
            for b in bs:
                b.r = list(inh)
            keep.extend(bs)
            self.live = keep
        return bs

    def dsem(self, sem):
        d = DSem(sem)
        self.dsems.append(d)
        return d

    def op(self, eng, fn, reads=(), writes=(), dsem=None):
        o = Op()
        o.eng = eng
        o.fn = fn
        o.dsem = dsem
        o.dcount = 0
        o.sig = False
        o.sigidx = 0
        deps = {}
        for b in reads:
            if b.w is not None:
                deps[id(b.w)] = b.w
        for b in writes:
            if b.w is not None:
                deps[id(b.w)] = b.w
            for r in b.r:
                deps[id(r)] = r
        if dsem is not None and dsem.last is not None:
            deps[id(dsem.last)] = dsem.last
        o.deps = [d for d in deps.values()
                  if not (d.dsem is None and d.eng == "pe" and eng == "pe") and d is not o]
        for b in reads:
            if not b.const:
                b.r.append(o)
        for b in writes:
            b.w = o
            b.r = []
        if dsem is not None:
            dsem.count += 16
            o.dcount = dsem.count
            dsem.last = o
        self.q[eng].append(o)
        return o

    def finalize(self):
        for e in self.ENGS:
            for o in self.q[e]:
                for d in o.deps:
                    if d.dsem is None:
                        d.sig = True
        for e in self.ENGS:
            c = 0
            for o in self.q[e]:
                if o.dsem is None and o.sig:
                    c += 1
                    o.sigidx = c

    def emit(self, ename, eng, esem, final_wait=False):
        waited = {}
        for o in self.q[ename]:
            need = {}
            for d in o.deps:
                if d.dsem is not None:
                    key = ("d", id(d.dsem))
                    sem = d.dsem.sem
                    val = d.dcount
                else:
                    key = ("e", d.eng)
                    sem = esem[d.eng]
                    val = d.sigidx
                if waited.get(key, 0) >= val:
                    continue
                if key not in need or need[key][1] < val:
                    need[key] = (sem, val)
            for key, (sem, val) in need.items():
                eng.wait_ge(sem, val)
                waited[key] = val
            ins = o.fn(eng)
            if o.dsem is not None:
                ins.then_inc(o.dsem.sem, 16)
            elif o.sig:
                ins.then_inc(esem[ename], 1)
        if final_wait:
            for d in self.dsems:
                if d.count > 0:
                    eng.wait_ge(d.sem, d.count)


class Ctx:
    pass


class Lay:
    def __init__(self, K):
        self.K = K
        self.off = 0

    def alloc(self, shape, dt, nb=1):
        K = self.K
        esz = 4 if dt == F32 else 2
        n = 1
        for s in shape:
            n *= s
        nbytes = (n * esz + 3) // 4 * 4
        o0 = self.off
        self.off += nbytes
        assert self.off <= K.arena_bytes, (self.off, K.arena_bytes)
        a = K.arena[:, o0 // 4:(o0 + nbytes) // 4]
        if dt != F32:
            a = a.bitcast(dt)
        a = a[:, :n]
        if len(shape) == 2:
            a = a.rearrange("p (a b) -> p a b", a=shape[0])
        elif len(shape) == 3:
            a = a.rearrange("p (a b c) -> p a b c", a=shape[0], b=shape[1])
        bufs = K.S.newbufs(nb, (o0, o0 + nbytes))
        return a, bufs


def split_tiles(xs, mx=4):
    n = len(xs)
    k = (n + mx - 1) // mx
    base, rem = divmod(n, k)
    out, i = [], 0
    for t in range(k):
        sz = base + (1 if t < rem else 0)
        out.append(xs[i:i + sz])
        i += sz
    return out


def plan_half(o0, o1):
    a1, b1 = max(o0 - 1, 0), o1 + 1
    a0, b0 = max(a1 - 1, 0), b1 + 1
    return dict(o=(o0, o1), r1=(a1, b1), r0=(a0, b0))


def emit_transposes(K, xi, yT_ap, yT_buf, pos, bp):
    S = K.S
    for kc in range(KC):
        bank = bp + kc // 4
        col = (kc % 4) * 128
        S.op("pe",
             (lambda e, bank=bank, col=col, kc=kc: e.transpose(
                 out=K.ps[bank][:, col:col + 128], in_=K.X[xi][:, kc * 128:(kc + 1) * 128],
                 identity=K.identf[:])),
             reads=[K.Xb[xi], K.cb], writes=[K.psb[bank]])
    src = K.psall[:, bp * 512:(bp + 2) * 512].rearrange("p (k n) -> p k n", k=KC)
    dst = yT_ap[:, :, pos * 128:(pos + 1) * 128]
    S.op("act", (lambda e: e.activation(out=dst, in_=src, func=AF.Copy)),
         reads=[K.psb[bp], K.psb[bp + 1]], writes=[yT_buf])


def dve_rsqrt(K, v, y, t, n, rbufs, wbuf, tbuf=None, iters=3):
    S = K.S
    yi = y.bitcast(I32)
    vi = v.bitcast(I32)
    S.op("dve", (lambda e: e.tensor_scalar(out=yi[:, :n], in0=vi[:, :n], scalar1=-0.5, scalar2=float(0x5f3759df),
                                           op0=ALU.mult, op1=ALU.add)),
         reads=list(rbufs), writes=[wbuf])
    wt = [wbuf] if tbuf is None else [wbuf, tbuf]
    for _ in range(iters):
        S.op("dve", (lambda e: e.tensor_tensor(out=t[:, :n], in0=y[:, :n], in1=y[:, :n], op=ALU.mult)),
             reads=[wbuf], writes=wt)
        S.op("dve", (lambda e: e.tensor_tensor(out=t[:, :n], in0=t[:, :n], in1=v[:, :n], op=ALU.mult)),
             reads=wt + list(rbufs), writes=wt)
        S.op("dve", (lambda e: e.tensor_scalar(out=t[:, :n], in0=t[:, :n], scalar1=-0.5, scalar2=1.5,
                                               op0=ALU.mult, op1=ALU.add)),
             reads=wt, writes=wt)
        S.op("dve", (lambda e: e.tensor_tensor(out=y[:, :n], in0=y[:, :n], in1=t[:, :n], op=ALU.mult)),
             reads=wt, writes=wt)


def ln_part1(K, xi, bp, lnt, slot, c, rs_=2.0):
    S = K.S
    X = K.X[xi]
    Xb = K.Xb[xi]
    st, mv, ve, y, t, lb = lnt[slot]
    for n in range(2):
        S.op("dve",
             (lambda e, n=n: e.scalar_tensor_tensor(
                 out=X[:, n * 512:(n + 1) * 512], in0=X[:, n * 512:(n + 1) * 512], scalar=rs_ * ALPHA,
                 in1=K.ps[bp + n][:, :], op0=ALU.mult, op1=ALU.add)),
             reads=[Xb, K.psb[bp + n]], writes=[Xb])
    for n in range(2):
        S.op("dve", (lambda e, n=n: e.bn_stats(out=st[:, c, n, :], in_=X[:, n * 512:(n + 1) * 512])),
             reads=[Xb], writes=[lb])
    S.op("dve", (lambda e: e.bn_aggr(out=mv[:, c, :], in_=st[:, c, :, :].rearrange("p a b -> p (a b)"))),
         reads=[lb], writes=[lb])


def ln_part2(K, xis, gbc, bbc, gbbufs, lnt, slot, rs_=2.0, store_gbs=None):
    S = K.S
    st, mv, ve, y, t, lb = lnt[slot]
    nb = len(xis)
    S.op("dve", (lambda e: e.tensor_scalar(out=ve[:, :nb], in0=mv[:, :nb, 1], scalar1=rs_ * rs_ * EPS, scalar2=None,
                                           op0=ALU.add)),
         reads=[lb], writes=[lb])
    dve_rsqrt(K, ve, y, t, nb, [lb], lb)
    for c, xi in enumerate(xis):
        X = K.X[xi]
        Xb = K.Xb[xi]
        S.op("dve", (lambda e, X=X, c=c: e.scalar_tensor_tensor(out=X[:, :], in0=X[:, :], scalar=mv[:, c, 0:1],
                                                                in1=gbc[:, :], op0=ALU.subtract, op1=ALU.mult)),
             reads=[Xb, lb, gbbufs[0]], writes=[Xb])
        S.op("dve", (lambda e, X=X, c=c: e.scalar_tensor_tensor(out=X[:, :], in0=X[:, :], scalar=y[:, c:c + 1],
                                                                in1=bbc[:, :], op0=ALU.mult, op1=ALU.add)),
             reads=[Xb, lb, gbbufs[1]], writes=[Xb])
        if store_gbs is not None:
            ds = K.out_sems[K.out_rr % len(K.out_sems)]
            K.out_rr += 1
            orow = store_gbs[c] * 128
            S.op("sp", (lambda e, X=X, orow=orow: e.dma_start(out=K.out_d[orow:orow + 128, :], in_=X[:, :])),
                 reads=[Xb], dsem=ds)


def alloc_ln(K, lay):
    lnt = []
    for s in range(2):
        st, b1 = lay.alloc([4, 2, 6], F32)
        mv, b2 = lay.alloc([4, 2], F32)
        ve, b3 = lay.alloc([4], F32)
        y, b4 = lay.alloc([4], F32)
        t, b5 = lay.alloc([4], F32)
        lnt.append((st, mv, ve, y, t, b1[0]))
    return lnt


def alloc_gb(K, lay):
    gbc, gb0 = lay.alloc([D], F32)
    bbc, gb1 = lay.alloc([D], F32)
    return gbc, bbc, [gb0[0], gb1[0]]


def emit_gb_load(K, gbc, bbc, gbb, l, which):
    S = K.S
    S.op("pool", (lambda e: e.dma_start(out=gbc[:, :], in_=K.lng_d[l, which, :].partition_broadcast(128))),
         writes=[gbb[0]], dsem=K.misc_sems[0])
    S.op("pool", (lambda e: e.dma_start(out=bbc[:, :], in_=K.lnb_d[l, which, :].partition_broadcast(128))),
         writes=[gbb[1]], dsem=K.misc_sems[1])


def ffn_step(K, l, which, xis, load_x, store_out):
    S = K.S
    wgu = K.wgu_d[which][l]
    wdn = K.wdn_d[which][l]
    lay = Lay(K)
    wg = [lay.alloc([KC, 2, 128], BF16, nb=2) for _ in range(4)]
    gbc, bbc, gbb = alloc_gb(K, lay)
    yT = [lay.alloc([KC, 512], BF16, nb=4) for _ in range(2)]
    h, hb = lay.alloc([NJ, 512], BF16, nb=NJ)
    wdr, wdb = lay.alloc([NJ, D], BF16, nb=NJ)
    sg = [lay.alloc([512], F32) for _ in range(2)]
    lnt = alloc_ln(K, lay)

    def emit_xloads(tb_):
        for xi in tb_:
            gb = K.xbase + xi
            ds = K.x_sems[xi % len(K.x_sems)]
            S.op("pool", (lambda e, xi=xi, gb=gb: e.dma_start(out=K.X[xi][:, :], in_=K.x_d[gb * 128:(gb + 1) * 128, :])),
                 writes=[K.Xb[xi]], dsem=ds)

    tiles = split_tiles(xis)
    if load_x:
        for tb_ in tiles[:2]:
            emit_xloads(tb_)
    wsrc = wgu.rearrange("(kc p) (two c) -> p kc two c", p=128, two=2)

    lq = [0]
    slot_of = {}

    def store_scr(key, dst, src_ap, src_bufs):
        sb = S.newbufs(1)[0]
        K.scr[key] = sb
        ds = K.st_sems[K.st_rr % len(K.st_sems)]
        K.st_rr += 1
        S.op("sp", (lambda e: e.dma_start(out=dst, in_=src_ap)), reads=src_bufs, writes=[sb], dsem=ds)

    def load_wg(j):
        slot = lq[0] % 4
        slot_of[j] = slot
        lq[0] += 1
        key = ("gu", which, l, j)
        flat = wg[slot][0].rearrange("p a b c -> p (a b c)")
        if key not in K.scr:
            for two in range(2):
                src = wsrc[:, :, two, j * 128:(j + 1) * 128]
                S.op("pool", (lambda e, two=two, src=src: e.dma_start(out=wg[slot][0][:, :, two, :], in_=src)),
                     writes=[wg[slot][1][two]], dsem=K.wg_sems[slot * 2 + two])
            store_scr(key, K.wgu_s[which][l][j], flat, wg[slot][1])
        else:
            src = K.wgu_s[which][l][j]
            S.op("sp", (lambda e: e.dma_start(out=flat, in_=src)), reads=[K.scr[key]],
                 writes=wg[slot][1], dsem=K.wg_sems[slot * 2])

    def load_wd(j):
        key = ("dn", which, l, j)
        if key not in K.scr:
            S.op("pool", (lambda e: e.dma_start(out=wdr[:, j, :], in_=wdn[j * 128:(j + 1) * 128, :])),
                 writes=[wdb[j]], dsem=K.wd_sems[j % len(K.wd_sems)])
            store_scr(key, K.wdn_s[which][l][j], wdr[:, j, :], [wdb[j]])
        else:
            src = K.wdn_s[which][l][j]
            S.op("pool", (lambda e: e.dma_start(out=wdr[:, j, :], in_=src)), reads=[K.scr[key]],
                 writes=[wdb[j]], dsem=K.wd_sems[j % len(K.wd_sems)])

    for p, xi in enumerate(tiles[0]):
        emit_transposes(K, xi, yT[0][0], yT[0][1][p], p, 4 + 2 * (p % 2))

    gq = 0
    pq = 0
    pending = []
    for ti, tb in enumerate(tiles):
        nb = len(tb)
        N = nb * 128
        ys, ysb = yT[ti % 2]
        nxt = tiles[ti + 1] if ti + 1 < len(tiles) else None
        if load_x and ti + 2 < len(tiles):
            emit_xloads(tiles[ti + 2])
        if ti == 0:
            load_wg(0)
            load_wg(1)
            load_wg(2)
        for j in range(NJ):
            slot = slot_of.pop(j)
            if j + 3 < NJ:
                load_wg(j + 3)
            elif nxt is not None:
                load_wg(j + 3 - NJ)
            if ti == 0:
                load_wd(j)
            bg, bu = ((0, 1), (2, 3), (4, 5))[gq % 3]
            sgt, sgb = sg[gq % 2]
            gq += 1
            for two, bank in ((0, bg), (1, bu)):
                for kc in range(KC):
                    S.op("pe",
                         (lambda e, two=two, bank=bank, kc=kc, slot=slot, N=N, ys=ys: e.matmul(
                             K.ps[bank][:, :N], lhsT=wg[slot][0][:, kc, two, :], rhs=ys[:, kc, :N],
                             start=(kc == 0), stop=(kc == KC - 1))),
                         reads=[wg[slot][1][two]] + ysb[:nb], writes=[K.psb[bank]])
            S.op("act", (lambda e, bg=bg, sgt=sgt, N=N: e.activation(out=sgt[:, :N], in_=K.ps[bg][:, :N], func=AF.Silu)),
                 reads=[K.psb[bg]], writes=[sgb[0]])
            S.op("dve", (lambda e, bu=bu, sgt=sgt, j=j, N=N: e.tensor_tensor(
                out=h[:, j, :N], in0=sgt[:, :N], in1=K.ps[bu][:, :N], op=ALU.mult)),
                 reads=[sgb[0], K.psb[bu]], writes=[hb[j]])
            if j == 2 and pending:
                pending.pop(0)()
            if nxt is not None and j in (3, 7, 11, 15) and not os.environ.get('NO_ILV'):
                p = (j - 3) // 4
                if p < len(nxt):
                    emit_transposes(K, nxt[p], yT[(ti + 1) % 2][0], yT[(ti + 1) % 2][1][p], p, 6)
        if nxt is not None and os.environ.get('NO_ILV'):
            for p in range(len(nxt)):
                emit_transposes(K, nxt[p], yT[(ti + 1) % 2][0], yT[(ti + 1) % 2][1][p], p, 6)
        if ti == 0:
            emit_gb_load(K, gbc, bbc, gbb, l, 0 if which == 0 else 2)
        for bi, xi in enumerate(tb):
            bp = 2 * (pq % 4)
            pq += 1
            for j in range(NJ):
                for n in range(2):
                    S.op("pe",
                         (lambda e, j=j, n=n, bi=bi, bp=bp: e.matmul(
                             K.ps[bp + n][:, :], lhsT=h[:, j, bi * 128:(bi + 1) * 128],
                             rhs=wdr[:, j, n * 512:(n + 1) * 512], start=(j == 0), stop=(j == NJ - 1))),
                         reads=[hb[j], wdb[j]], writes=[K.psb[bp + n]])
            ln_part1(K, xi, bp, lnt, ti % 2, bi)
        pending.append(lambda tb=tb, ti=ti: ln_part2(
            K, tb, gbc, bbc, gbb, lnt, ti % 2,
            store_gbs=([K.xbase + xi - K.out_base for xi in tb] if store_out else None)))
    while pending:
        pending.pop(0)()


def mixer_step(K, l, in_xis, out_xis, true_edge):
    S = K.S
    lay = Lay(K)
    win = [lay.alloc([KC, 128], BF16) for _ in range(4)]
    rope, ropeb = lay.alloc([2, 512], F32)
    tmp = [lay.alloc([512], F32) for _ in range(2)]
    assert lay.off == 16384
    gbc, bbc, gbb = alloc_gb(K, lay)
    tmp += [lay.alloc([512], F32) for _ in range(2)]
    yT, yTb = lay.alloc([KC, 512], BF16, nb=4)
    qrot = [lay.alloc([4, 512], BF16) for _ in range(2)]
    kk = [lay.alloc([2, 768], BF16) for _ in range(2)]
    vv = [lay.alloc([6, 128], BF16) for _ in range(2)]
    uu = [lay.alloc([2, 544], BF16) for _ in range(2)]
    scg = [lay.alloc([2, 544], F32) for _ in range(2)]
    scb = [lay.alloc([2, 512], F32) for _ in range(2)]
    PT = [lay.alloc([512], BF16) for _ in range(4)]
    attT, attb = lay.alloc([4, 512], BF16, nb=4)
    ysc, yscb = lay.alloc([2, 512], BF16)
    ycc, yccb = lay.alloc([2, 512], BF16)
    dgrp = [lay.alloc([8, 128], BF16) for _ in range(3)]
    vc, vcb = lay.alloc([2, 512], F32, nb=2)
    sq, sqb = lay.alloc([2, 512], F32, nb=2)
    dn, dnb = lay.alloc([512], F32)
    rc, rcb = lay.alloc([512], F32)
    oc, ocb = lay.alloc([512], F32)
    wout, woutb = lay.alloc([KC, D], BF16, nb=KC)
    lnt = alloc_ln(K, lay)
    es, esb = lay.alloc([4], F32)

    pp0 = l * NPPL
    scw = lambda c, k: K.pp[:, pp0 + c * 3 + k: pp0 + c * 3 + k + 1]
    ccw = lambda c, k: K.pp[:, pp0 + 6 + c * 31 + k: pp0 + 6 + c * 31 + k + 1]
    ccb = lambda c: K.pp[:, pp0 + 68 + c: pp0 + 69 + c]
    ccg = lambda c: K.pp[:, pp0 + 70 + c: pp0 + 71 + c]
    ccbe = lambda c: K.pp[:, pp0 + 72 + c: pp0 + 73 + c]

    S.op("act", (lambda e: e.activation(out=es[:, :], in_=K.pp[:, pp0 + 74:pp0 + 78], func=AF.Exp)),
         reads=[K.cb], writes=[esb[0]])
    wo_src = K.wout_d[l]

    def store_scr(key, dst, src_ap, src_bufs):
        sb = S.newbufs(1)[0]
        K.scr[key] = sb
        ds = K.st_sems[K.st_rr % len(K.st_sems)]
        K.st_rr += 1
        S.op("sp", (lambda e: e.dma_start(out=dst, in_=src_ap)), reads=src_bufs, writes=[sb], dsem=ds)

    def load_wout_gb():
        for kc in range(KC):
            key = ("wo", l, kc)
            if key not in K.scr:
                S.op("pool", (lambda e, kc=kc: e.dma_start(out=wout[:, kc, :], in_=wo_src[kc * 128:(kc + 1) * 128, :])),
                     writes=[woutb[kc]], dsem=K.wd_sems[kc % len(K.wd_sems)])
                store_scr(key, K.wout_s[l][kc], wout[:, kc, :], [woutb[kc]])
            else:
                S.op("pool", (lambda e, kc=kc: e.dma_start(out=wout[:, kc, :], in_=K.wout_s[l][kc])),
                     reads=[K.scr[key]], writes=[woutb[kc]], dsem=K.wd_sems[kc % len(K.wd_sems)])
        emit_gb_load(K, gbc, bbc, gbb, l, 1)

    wi_src = K.win_d[l].rearrange("(kc p) c -> p kc c", p=128)

    tiles = split_tiles(in_xis)
    nt = len(tiles)
    out_set = set(out_xis)
    first_in = in_xis[0]
    st = dict(wq=0, pq=0, tq=0, ptq=0, dq=0, ab=0, sb=0)

    def load_win(ci):
        slot = st["wq"] % 4
        st["wq"] += 1
        key = ("wi", l, ci)
        flat = win[slot][0].rearrange("p a b -> p (a b)")
        if key not in K.scr:
            src = wi_src[:, :, ci * 128:(ci + 1) * 128]
            S.op("pool", (lambda e: e.dma_start(out=win[slot][0][:, :, :], in_=src)),
                 writes=[win[slot][1][0]], dsem=K.wi_sems[slot])
            store_scr(key, K.win_s[l][ci], flat, [win[slot][1][0]])
        else:
            src = K.win_s[l][ci]
            S.op("sp", (lambda e: e.dma_start(out=flat, in_=src)), reads=[K.scr[key]],
                 writes=[win[slot][1][0]], dsem=K.wi_sems[slot])
        return slot

    def z_chunk(ci, slot, bank, ys_N, nb):
        N = ys_N
        for kc in range(KC):
            S.op("pe", (lambda e, kc=kc: e.matmul(K.ps[bank][:, :N], lhsT=win[slot][0][:, kc, :], rhs=yT[:, kc, :N],
                                                  start=(kc == 0), stop=(kc == KC - 1))),
                 reads=[win[slot][1][0]] + yTb[:nb], writes=[K.psb[bank]])

    def gettmp():
        t = tmp[st["tq"] % 4]
        st["tq"] += 1
        return t

    CH = [("scg", 0, (2, 4)), ("scg", 1, (3, 5)), ("scb", 0, (0,)), ("scb", 1, (1,)),
          ("q", 0, (6,)), ("q", 1, (7,)), ("q", 2, (8,)), ("q", 3, (9,)),
          ("k", 0, (10,)), ("k", 1, (11,)), ("u", 0, (12, 14)), ("u", 1, (13, 15))]
    VCH = 16
    flat = [VCH] + [ci for _, _, cs in CH for ci in cs]

    def phase_A(t):
        tb = tiles[t]
        nb = len(tb)
        N = nb * 128
        buf = t % 2
        pb = (t - 1) % 2
        gb0 = K.xbase + tb[0]
        kk_a, kk_b = kk[buf][0], kk[buf][1][0]
        vv_a, vv_b = vv[buf][0], vv[buf][1][0]
        uu_a, uu_b = uu[buf][0], uu[buf][1][0]
        sg_a, sg_b = scg[buf][0], scg[buf][1][0]
        sb_a, sb_b = scb[buf][0], scb[buf][1][0]
        q_a, q_b = qrot[buf][0], qrot[buf][1][0]
        nbp = len(tiles[t - 1]) if t > 0 else 0
        if t > 0:
            S.op("pool", (lambda e: e.tensor_copy(out=kk_a[:, :, 0:128], in_=kk[pb][0][:, :, nbp * 128:(nbp + 1) * 128])),
                 reads=[kk[pb][1][0]], writes=[kk_b])
            S.op("pool", (lambda e: e.tensor_copy(out=vv_a[:, 0, :], in_=vv[pb][0][:, nbp, :])),
                 reads=[vv[pb][1][0]], writes=[vv_b])
            S.op("pool", (lambda e: e.tensor_copy(out=uu_a[:, :, 0:16], in_=uu[pb][0][:, :, nbp * 128:nbp * 128 + 16])),
                 reads=[uu[pb][1][0]], writes=[uu_b])
            S.op("pool", (lambda e: e.tensor_copy(out=sg_a[:, :, 0:16], in_=scg[pb][0][:, :, nbp * 128:nbp * 128 + 16])),
                 reads=[scg[pb][1][0]], writes=[sg_b])
        else:
            S.op("pool", (lambda e: e.memset(uu_a[:, :, 0:16], 0.0)), writes=[uu_b])
            S.op("pool", (lambda e: e.memset(sg_a[:, :, 0:16], 0.0)), writes=[sg_b])
        S.op("sp", (lambda e: e.dma_start(out=rope[:, :, :N], in_=K.rope_d[:, :, gb0 * 128:gb0 * 128 + N])),
             writes=[ropeb[0]], dsem=K.misc_sems[2])
        slots = {}
        pend = list(flat)
        for _ in range(3):
            ci = pend.pop(0)
            slots[ci] = load_win(ci)

        def nextload():
            if pend:
                ci = pend.pop(0)
                slots[ci] = load_win(ci)

        nextload()
        vslot = slots[VCH]
        for bi in range(nb):
            for kc in range(KC):
                S.op("pe", (lambda e, bi=bi, kc=kc: e.matmul(
                    K.ps[4][:, bi * 128:(bi + 1) * 128], lhsT=yT[:, kc, bi * 128:(bi + 1) * 128],
                    rhs=win[vslot][0][:, kc, :], start=(kc == 0), stop=(kc == KC - 1))),
                     reads=[win[vslot][1][0], yTb[bi]], writes=[K.psb[4]])
        S.op("act", (lambda e: e.activation(
            out=vv_a[:, 1:1 + nb, :], in_=K.ps[4][:, :N].rearrange("p (a b) -> p a b", a=nb), func=AF.Copy)),
             reads=[K.psb[4]], writes=[vv_b])
        if t > 0:
            S.op("pool", (lambda e: e.tensor_copy(out=vv[pb][0][:, nbp + 1, :], in_=vv_a[:, 1, :])),
                 reads=[vv_b], writes=[vv[pb][1][0]])
        for chi, (kind, c, cs) in enumerate(CH):
            if chi == 8:
                yield
            if len(cs) == 2:
                st["ab"] = (st["ab"] + 1) // 2 * 2
                pair = (st["ab"] % 4, st["ab"] % 4 + 1)
                st["ab"] += 2
            else:
                pair = (st["ab"] % 4, None)
                st["ab"] += 1
            for i, ci in enumerate(cs):
                nextload()
                z_chunk(ci, slots[ci], pair[i], N, nb)
            b0 = pair[0]
            b1 = pair[1]
            if kind == "scg":
                t1, t1b = gettmp()
                S.op("act", (lambda e, t1=t1, b0=b0: e.activation(out=t1[:, :N], in_=K.ps[b0][:, :N], func=AF.Copy)),
                     reads=[K.psb[b0]], writes=[t1b[0]])
                S.op("dve", (lambda e, t1=t1, b1=b1, c=c: e.tensor_tensor(
                    out=sg_a[:, c, 16:16 + N], in0=t1[:, :N], in1=K.ps[b1][:, :N], op=ALU.mult)),
                     reads=[t1b[0], K.psb[b1]], writes=[sg_b])
            elif kind == "scb":
                S.op("act", (lambda e, b0=b0, c=c: e.activation(out=sb_a[:, c, :N], in_=K.ps[b0][:, :N], func=AF.Copy)),
                     reads=[K.psb[b0]], writes=[sb_b])
            elif kind in ("q", "k"):
                t1, t1b = gettmp()
                t2, t2b = gettmp()
                S.op("act", (lambda e, t1=t1, b0=b0: e.activation(out=t1[:, :N], in_=K.ps[b0][:, :N], func=AF.Copy)),
                     reads=[K.psb[b0]], writes=[t1b[0]])
                for sblk in range(4):
                    dblk = sblk ^ 1
                    if sblk < 3:
                        S.op("act", (lambda e, t1=t1, t2=t2, sblk=sblk, dblk=dblk: e.activation(
                            out=t2[dblk * 32:(dblk + 1) * 32, :N], in_=t1[sblk * 32:(sblk + 1) * 32, :N],
                            func=AF.Copy)),
                             reads=[t1b[0]], writes=[t2b[0]])
                    else:
                        S.op("dve", (lambda e, t1=t1, t2=t2, sblk=sblk, dblk=dblk: e.tensor_copy(
                            out=t2[dblk * 32:(dblk + 1) * 32, :N], in_=t1[sblk * 32:(sblk + 1) * 32, :N])),
                             reads=[t1b[0]], writes=[t2b[0]])
                S.op("dve", (lambda e, t1=t1: e.tensor_tensor(
                    out=t1[:, :N], in0=t1[:, :N], in1=rope[:, 0, :N], op=ALU.mult)),
                     reads=[t1b[0], ropeb[0]], writes=[t1b[0]])
                S.op("dve", (lambda e, t2=t2: e.tensor_tensor(
                    out=t2[:, :N], in0=t2[:, :N], in1=rope[:, 1, :N], op=ALU.mult)),
                     reads=[t2b[0], ropeb[0]], writes=[t2b[0]])
                if kind == "q":
                    S.op("pool", (lambda e, t1=t1, t2=t2, c=c: e.tensor_tensor(
                        out=q_a[:, c, :N], in0=t1[:, :N], in1=t2[:, :N], op=ALU.add)),
                         reads=[t1b[0], t2b[0]], writes=[q_b])
                else:
                    S.op("pool", (lambda e, t1=t1, t2=t2, c=c: e.tensor_tensor(
                        out=kk_a[:, c, 128:128 + N], in0=t1[:, :N], in1=t2[:, :N], op=ALU.add)),
                         reads=[t1b[0], t2b[0]], writes=[kk_b])
            elif kind == "u":
                t1, t1b = gettmp()
                S.op("act", (lambda e, t1=t1, b1=b1: e.activation(out=t1[:, :N], in_=K.ps[b1][:, :N], func=AF.Tanh,
                                                                   scale=0.5)),
                     reads=[K.psb[b1]], writes=[t1b[0]])
                S.op("dve", (lambda e, t1=t1, b0=b0, c=c: e.scalar_tensor_tensor(
                    out=uu_a[:, c, 16:16 + N], in0=t1[:, :N], scalar=1.0, in1=K.ps[b0][:, :N],
                    op0=ALU.add, op1=ALU.mult)),
                     reads=[t1b[0], K.psb[b0]], writes=[uu_b])
        yield
        if t > 0:
            S.op("pool", (lambda e: e.tensor_copy(out=kk[pb][0][:, :, (nbp + 1) * 128:(nbp + 2) * 128], in_=kk_a[:, :, 128:256])),
                 reads=[kk_b], writes=[kk[pb][1][0]])
            S.op("pool", (lambda e: e.tensor_copy(out=uu[pb][0][:, :, 16 + nbp * 128:32 + nbp * 128], in_=uu_a[:, :, 16:32])),
                 reads=[uu_b], writes=[uu[pb][1][0]])
            S.op("pool", (lambda e: e.tensor_copy(out=scg[pb][0][:, :, 16 + nbp * 128:32 + nbp * 128], in_=sg_a[:, :, 16:32])),
                 reads=[sg_b], writes=[scg[pb][1][0]])

    def transposes_for(t):
        tb = tiles[t]
        for p, xi in enumerate(tb):
            emit_transposes(K, xi, yT, yTb[p], p, 4 + 2 * (p % 2))

    def phase_B1(t, next_tr=None):
        tb = tiles[t]
        buf = t % 2
        kk_a, kk_b = kk[buf][0], kk[buf][1][0]
        vv_a, vv_b = vv[buf][0], vv[buf][1][0]
        uu_a, uu_b = uu[buf][0], uu[buf][1][0]
        sg_a, sg_b = scg[buf][0], scg[buf][1][0]
        sb_a, sb_b = scb[buf][0], scb[buf][1][0]
        q_a, q_b = qrot[buf][0], qrot[buf][1][0]
        outs = [(k, xi) for k, xi in enumerate(tb) if xi in out_set]
        if not outs:
            return
        k0 = outs[0][0]
        nob = len(outs)
        NO = nob * 128
        o0 = k0 * 128
        its = []
        for bidx, (k, xi) in enumerate(outs):
            kbs = []
            if not (xi == first_in):
                kbs.append((k, 0))
            kbs.append((k + 1, None))
            kbs.append((k + 2, 1))
            for g in range(2):
                for ii, (kp, mk) in enumerate(kbs):
                    its.append(dict(k=k, g=g, ii=ii, kp=kp, mk=mk, nk=len(kbs), ob=0,
                                    last=(g == 1 and ii == len(kbs) - 1)))

        def emit_scores(it):
            sa, sbk = ((2, 3), (4, 5), (6, 7))[st["sb"] % 3]
            st["sb"] += 1
            it["sa"], it["sbk"] = sa, sbk
            it["pt"], it["ptb"] = PT[st["ptq"] % 4]
            st["ptq"] += 1
            k, g, kp, mk = it["k"], it["g"], it["kp"], it["mk"]
            if mk is not None:
                for hi, bank in ((0, sa), (1, sbk)):
                    S.op("pe", (lambda e, bank=bank, mk=mk: e.matmul(
                        K.ps[bank][:, 0:256], lhsT=K.identb[:, :], rhs=K.maskneg[:, mk, :],
                        start=True, stop=False)),
                         reads=[K.cb], writes=[K.psb[bank]])
            for hi, bank in ((0, sa), (1, sbk)):
                S.op("pe", (lambda e, hi=hi, bank=bank, kp=kp, g=g, k=k, mk=mk: e.matmul(
                    K.ps[bank][:, 0:256].rearrange("p (a b) -> p a b", a=2),
                    lhsT=kk_a[hi * 64:(hi + 1) * 64, g, kp * 128:(kp + 1) * 128],
                    rhs=q_a[hi * 64:(hi + 1) * 64, 2 * g:2 * g + 2, k * 128:(k + 1) * 128],
                    start=(mk is None), stop=True)),
                     reads=[kk_b, q_b], writes=[K.psb[bank]])

        def emit_soft(it):
            pt, ptb = it["pt"], it["ptb"]
            for hi, bank in ((0, it["sa"]), (1, it["sbk"])):
                S.op("act", (lambda e, hi=hi, bank=bank, pt=pt: e.activation(
                    out=pt[:, hi * 256:(hi + 1) * 256], in_=K.ps[bank][:, 0:256], func=AF.Exp, scale=0.125)),
                     reads=[K.psb[bank]], writes=[ptb[0]])

        def emit_pv(it):
            pt, ptb, kp, g, ii, nk, ob, k = it["pt"], it["ptb"], it["kp"], it["g"], it["ii"], it["nk"], it["ob"], it["k"]
            S.op("pe", (lambda e: e.matmul(
                K.ps[ob][g * 64:(g + 1) * 64, :], lhsT=vv_a[:, kp, g * 64:(g + 1) * 64], rhs=pt[:, :],
                start=(ii == 0), stop=(ii == nk - 1))),
                 reads=[vv_b, ptb[0]], writes=[K.psb[ob]])
            S.op("pe", (lambda e: e.matmul(
                K.ps[ob + 1][g * 64:(g + 1) * 64, :], lhsT=K.onesb[:, 0:64], rhs=pt[:, :],
                start=(ii == 0), stop=(ii == nk - 1))),
                 reads=[K.cb, ptb[0]], writes=[K.psb[ob + 1]])
            if it["last"]:
                S.op("act", (lambda e: e.activation(out=oc[:, :], in_=K.ps[ob][:, :], func=AF.Copy)),
                     reads=[K.psb[ob]], writes=[ocb[0]])
                S.op("act", (lambda e: e.activation(out=dn[:, :], in_=K.ps[ob + 1][:, :], func=AF.Copy)),
                     reads=[K.psb[ob + 1]], writes=[dnb[0]])
                for e4 in range(4):
                    S.op("dve", (lambda e, e4=e4: e.tensor_scalar(
                        out=dn[:, e4 * 128:(e4 + 1) * 128], in0=dn[:, e4 * 128:(e4 + 1) * 128],
                        scalar1=es[:, e4:e4 + 1], scalar2=None, op0=ALU.add)),
                         reads=[esb[0], dnb[0]], writes=[dnb[0]])
                S.op("dve", (lambda e: e.reciprocal(out=rc[:, :], in_=dn[:, :])), reads=[dnb[0]], writes=[rcb[0]])
                S.op("dve", (lambda e: e.tensor_tensor(
                    out=attT[:, :, k * 128:(k + 1) * 128], in0=oc[:, :].rearrange("p (a b) -> p a b", a=4),
                    in1=rc[:, :].rearrange("p (a b) -> p a b", a=4), op=ALU.mult)),
                     reads=[ocb[0], rcb[0]], writes=[attb[k]])

        for c in range(2):
            a1, a1b = gettmp()
            base = 16 + o0
            S.op("dve", (lambda e, a1=a1, c=c, base=base: e.tensor_scalar(
                out=a1[:, :NO], in0=sg_a[:, c, base - 1:base - 1 + NO], scalar1=scw(c, 0), scalar2=None,
                op0=ALU.mult)),
                 reads=[sg_b, K.cb], writes=[a1b[0]])
            for kt in (1, 2):
                S.op("dve", (lambda e, a1=a1, c=c, base=base, kt=kt: e.scalar_tensor_tensor(
                    out=a1[:, :NO], in0=sg_a[:, c, base - 1 + kt:base - 1 + kt + NO], scalar=scw(c, kt),
                    in1=a1[:, :NO], op0=ALU.mult, op1=ALU.add)),
                     reads=[sg_b, K.cb, a1b[0]], writes=[a1b[0]])
            S.op("dve", (lambda e, a1=a1, c=c: e.tensor_tensor(
                out=ysc[:, c, o0:o0 + NO], in0=a1[:, :NO], in1=sb_a[:, c, o0:o0 + NO], op=ALU.mult)),
                 reads=[a1b[0], sb_b], writes=[yscb[0]])
        while pendB:
            pendB.pop(0)()
        st["sb"] = 0
        for i0 in range(min(2, len(its))):
            emit_scores(its[i0])
        for i, it in enumerate(its):
            if i + 2 < len(its):
                emit_scores(its[i + 2])
            emit_soft(it)
            emit_pv(it)
        if next_tr is not None:
            transposes_for(next_tr)
        for c in range(2):
            bank = 6 + c
            for g0 in range(0, 31, 8):
                ng = min(8, 31 - g0)
                dg, dgb = dgrp[st["dq"] % 3]
                st["dq"] += 1
                col = pp0 + 6 + c * 31 + g0
                S.op("pool", (lambda e, dg=dg, ng=ng, col=col: e.tensor_tensor(
                    out=dg[:, :ng, :], in0=K.identb[:, :].unsqueeze(1).broadcast_to([128, ng, 128]),
                    in1=K.pp[:, col:col + ng].unsqueeze(2).broadcast_to([128, ng, 128]), op=ALU.mult)),
                     reads=[K.cb], writes=[dgb[0]])
                for i in range(ng):
                    kt = g0 + i
                    base = 16 + o0 + kt - 15
                    S.op("pe", (lambda e, dg=dg, i=i, c=c, kt=kt, base=base, bank=bank: e.matmul(
                        K.ps[bank][:, :NO], lhsT=dg[:, i, :], rhs=uu_a[:, c, base:base + NO],
                        start=(kt == 0), stop=(kt == 30))),
                         reads=[dgb[0], uu_b], writes=[K.psb[bank]])
            S.op("act", (lambda e, c=c, bank=bank: e.activation(
                out=vc[:, c, :NO], in_=K.ps[bank][:, :NO], func=AF.Identity, bias=ccb(c), scale=0.5)),
                 reads=[K.psb[bank], K.cb], writes=[vcb[c]])
            S.op("act", (lambda e, c=c: e.activation(out=sq[:, c, :NO], in_=vc[:, c, :NO], func=AF.Square)),
                 reads=[vcb[c]], writes=[sqb[c]])
        for c in range(2):
            S.op("pe", (lambda e, c=c: e.matmul(K.ps[6][:, :NO], lhsT=K.onesf[:, :], rhs=vc[:, c, :NO],
                                                start=(c == 0), stop=(c == 1))),
                 reads=[K.cb, vcb[c]], writes=[K.psb[6]])
        for c in range(2):
            S.op("pe", (lambda e, c=c: e.matmul(K.ps[7][:, :NO], lhsT=K.onesf[:, :], rhs=sq[:, c, :NO],
                                                start=(c == 0), stop=(c == 1))),
                 reads=[K.cb, sqb[c]], writes=[K.psb[7]])
        m2, m2b = sq[:, 0, :], sqb[0]
        rsd, rsdb = sq[:, 1, :], sqb[1]
        mt, mtb = gettmp()
        S.op("act", (lambda e, mt=mt: e.activation(out=mt[:, :NO], in_=K.ps[6][:, :NO], func=AF.Copy)),
             reads=[K.psb[6]], writes=[mtb[0]])
        S.op("dve", (lambda e, mt=mt: e.tensor_tensor(out=m2[:, :NO], in0=mt[:, :NO], in1=mt[:, :NO], op=ALU.mult)),
             reads=[mtb[0]], writes=[m2b])
        S.op("dve", (lambda e: e.scalar_tensor_tensor(out=m2[:, :NO], in0=K.ps[7][:, :NO], scalar=EPS, in1=m2[:, :NO],
                                                      op0=ALU.add, op1=ALU.subtract)),
             reads=[K.psb[7], m2b], writes=[m2b])
        rt, rtb = gettmp()
        dve_rsqrt(K, m2, rsd, rt, NO, [m2b], rsdb, tbuf=rtb[0], iters=2)
        for c in range(2):
            S.op("dve", (lambda e, c=c, mt=mt: e.tensor_tensor(out=vc[:, c, :NO], in0=vc[:, c, :NO], in1=mt[:, :NO],
                                                               op=ALU.subtract)),
                 reads=[vcb[c], mtb[0]], writes=[vcb[c]])
            S.op("dve", (lambda e, c=c: e.tensor_tensor(out=vc[:, c, :NO], in0=vc[:, c, :NO], in1=rsd[:, :NO],
                                                        op=ALU.mult)),
                 reads=[vcb[c], rsdb], writes=[vcb[c]])

    def emit_silu(t):
        tb = tiles[t]
        outs = [(k, xi) for k, xi in enumerate(tb) if xi in out_set]
        if not outs:
            return
        o0 = outs[0][0] * 128
        NO = len(outs) * 128
        for c in range(2):
            S.op("act", (lambda e, c=c: e.activation(out=ycc[:, c, o0:o0 + NO], in_=vc[:, c, :NO], func=AF.Silu,
                                                     bias=ccbe(c), scale=ccg(c))),
                 reads=[vcb[c], K.cb], writes=[yccb[0]])

    def phase_B2(t):
        tb = tiles[t]
        outs = [(k, xi) for k, xi in enumerate(tb) if xi in out_set]
        if not outs:
            return
        k0 = outs[0][0]
        nob = len(outs)
        NO = nob * 128
        o0 = k0 * 128
        for k, xi in outs:
            bp = 4 if st["pq"] % 2 == 0 else 6
            st["pq"] += 1
            for kc in range(KC):
                if kc < 2:
                    src, sbuf = ysc[:, kc, k * 128:(k + 1) * 128], yscb[0]
                elif kc < 6:
                    src, sbuf = attT[:, kc - 2, k * 128:(k + 1) * 128], attb[k]
                else:
                    src, sbuf = ycc[:, kc - 6, k * 128:(k + 1) * 128], yccb[0]
                for n in range(2):
                    S.op("pe", (lambda e, src=src, kc=kc, n=n, bp=bp: e.matmul(
                        K.ps[bp + n][:, :], lhsT=src, rhs=wout[:, kc, n * 512:(n + 1) * 512],
                        start=(kc == 0), stop=(kc == KC - 1))),
                         reads=[sbuf, woutb[kc]], writes=[K.psb[bp + n]])
            ln_part1(K, xi, bp, lnt, t % 2, k - k0, rs_=1.0)
        pendB.append(lambda outs=outs, t=t: ln_part2(K, [xi for _, xi in outs], gbc, bbc, gbb, lnt, t % 2, rs_=1.0))

    pendB = []

    def run_all(g):
        for _ in g:
            pass

    transposes_for(0)
    run_all(phase_A(0))
    if nt > 1:
        transposes_for(1)
        run_all(phase_A(1))
    load_wout_gb()
    for t in range(nt):
        phase_B1(t, next_tr=(t + 2 if t + 2 < nt else None))
        if t + 2 < nt:
            g = phase_A(t + 2)
            next(g)
            emit_silu(t)
            next(g)
            phase_B2(t)
            run_all(g)
        else:
            emit_silu(t)
            phase_B2(t)
    while pendB:
        pendB.pop(0)()


def build(nb_tok, halves=None, dbg_steps=None):
    nb_in = nb_tok + 2
    T_in = nb_in * 128
    if halves is None:
        hh = nb_tok // 2
        halves = [(0, hh), (hh, nb_tok)]
    nx = max(halves[0][1] + 2, nb_tok + 2 - (halves[0][1] - 1))

    nc = bass.Bass("TRN2", target_bir_lowering=False)
    K = Ctx()
    K.nc = nc
    K.S = Sched()
    S = K.S
    dt = lambda name, shape, dtype, kind="ExternalInput": nc.dram_tensor(name, shape, dtype, kind=kind).ap()
    K.x_d = dt("x", [T_in, D], F32)
    K.wgu_d = [dt("wgu1", [DEPTH, D, 2 * DFF], F32), dt("wgu2", [DEPTH, D, 2 * DFF], F32)]
    K.wdn_d = [dt("wdn1", [DEPTH, DFF, D], F32), dt("wdn2", [DEPTH, DFF, D], F32)]
    K.win_d = dt("win", [DEPTH, D, NWIN * 128], F32)
    K.wout_d = dt("wout", [DEPTH, D, D], F32)
    K.lng_d = dt("lng", [DEPTH, 3, D], F32)
    K.lnb_d = dt("lnb", [DEPTH, 3, D], F32)
    pp_d = dt("pp", [128, DEPTH * NPPL], F32)
    K.rope_d = dt("rope", [128, 2, T_in], F32)
    cf_d = dt("cstf", [128, 256], F32)
    cb_d = dt("cstb", [128, 128 + 64 + 512], BF16)
    K.out_d = dt("out", [nb_tok * 128, D], F32, kind="ExternalOutput")
    K.wgu_s = [[dt(f"wgus{w}{l}", [NJ, 128, KC * 2 * 128], BF16, kind="Internal") for l in range(DEPTH)] for w in range(2)]
    K.wdn_s = [[dt(f"wdns{w}{l}", [NJ, 128, D], BF16, kind="Internal") for l in range(DEPTH)] for w in range(2)]
    K.win_s = [dt(f"wins{l}", [NWIN, 128, KC * 128], BF16, kind="Internal") for l in range(DEPTH)]
    K.wout_s = [dt(f"wouts{l}", [KC, 128, D], BF16, kind="Internal") for l in range(DEPTH)]
    K.scr = {}
    K.st_rr = 0
    K.stash_d = [dt(f"stash{i}", [2, 128, D], F32, kind="Internal") for i in range(2)]

    from contextlib import ExitStack
    with ExitStack() as es:
        sb = lambda name, shape, dtype: es.enter_context(nc.sbuf_tensor(name, shape, dtype))
        Xall = sb("Xall", [128, nx, D], F32)
        K.X = [Xall[:, i, :] for i in range(nx)]
        K.Xb = S.newbufs(nx)
        cstf = sb("cstf_sb", [128, 256], F32)
        cstb = sb("cstb_sb", [128, 128 + 64 + 512], BF16)
        K.pp = sb("pp_sb", [128, DEPTH * NPPL], F32)
        K.identf = cstf[:, 0:128]
        K.onesf = cstf[:, 128:256]
        K.identb = cstb[:, 0:128]
        K.onesb = cstb[:, 128:128 + 64]
        K.maskneg = cstb[:, 128 + 64:128 + 64 + 512].rearrange("p (a b) -> p a b", a=2)
        K.cb = S.newbufs(1)[0]
        rem = nc.sbuf_bytes_remaining
        K.arena_bytes = (rem - 64) // 4 * 4
        K.arena = sb("arena", [128, K.arena_bytes // 4], F32)
        K.psall = es.enter_context(nc.psum_tensor("psall", [128, 4096], F32))
        K.ps = [K.psall[:, i * 512:(i + 1) * 512] for i in range(8)]
        K.psb = S.newbufs(8)
        sem = lambda name: es.enter_context(nc.semaphore(name))
        esem = {e: sem("e_" + e) for e in Sched.ENGS}
        K.wg_sems = [S.dsem(sem(f"wg{i}")) for i in range(8)]
        K.st_sems = [S.dsem(sem(f"st{i}")) for i in range(8)]
        K.wi_sems = [S.dsem(sem(f"wi{i}")) for i in range(4)]
        K.wd_sems = [S.dsem(sem(f"wd{i}")) for i in range(6)]
        K.x_sems = [S.dsem(sem(f"xs{i}")) for i in range(4)]
        K.out_sems = [S.dsem(sem(f"os{i}")) for i in range(4)]
        K.misc_sems = [S.dsem(sem(f"ms{i}")) for i in range(4)]
        K.out_rr = 0
        K.out_base = 0

        cbs = S.newbufs(4)
        K.cjoin = sb("cjoin", [128, 2], F32)
        S.op("sp", (lambda e: e.dma_start(out=cstf[:, :], in_=cf_d[:, :])), writes=[cbs[0]], dsem=K.misc_sems[3])
        S.op("sp", (lambda e: e.dma_start(out=cstb[:, :], in_=cb_d[:, :])), writes=[cbs[1]], dsem=K.misc_sems[3])
        S.op("sp", (lambda e: e.dma_start(out=K.pp[:, :], in_=pp_d[:, :])), writes=[cbs[2]], dsem=K.misc_sems[3])
        S.op("pool", (lambda e: e.memset(K.cjoin[:, :], 0.0)), writes=[cbs[3]])
        S.op("pool", (lambda e: e.memset(K.cjoin[:, :], 0.0)), reads=cbs, writes=[K.cb])
        K.cb.const = True

        assert len(halves) == 2 and halves[0][0] == 0 and halves[0][1] == halves[1][0] and halves[1][1] == nb_tok
        h, n = halves[0][1], nb_tok
        stash_b = {1: S.newbufs(2), 2: S.newbufs(2)}
        blk_of = {1: [h, h + 1], 2: [h - 1, h]}

        def rng(a, b):
            return list(range(a - K.xbase, b - K.xbase))

        def stash(which, store):
            for i, blk in enumerate(blk_of[which]):
                xi = blk - K.xbase
                ds = K.x_sems[i % len(K.x_sems)]
                if store:
                    S.op("sp", (lambda e, xi=xi, i=i: e.dma_start(out=K.stash_d[which - 1][i], in_=K.X[xi][:, :])),
                         reads=[K.Xb[xi]], writes=[stash_b[which][i]], dsem=ds)
                else:
                    S.op("sp", (lambda e, xi=xi, i=i: e.dma_start(out=K.X[xi][:, :], in_=K.stash_d[which - 1][i])),
                         reads=[stash_b[which][i]], writes=[K.Xb[xi]], dsem=ds)

        K.xbase = 0
        ffn_step(K, 0, 0, rng(0, h + 2), load_x=True, store_out=False)
        stash(1, True)
        mixer_step(K, 0, rng(0, h + 2), rng(0, h + 1), True)
        ffn_step(K, 0, 1, rng(0, h + 1), load_x=False, store_out=False)
        ffn_step(K, 1, 0, rng(0, h + 1), load_x=False, store_out=False)
        stash(2, True)
        mixer_step(K, 1, rng(0, h + 1), rng(0, h), True)
        ffn_step(K, 1, 1, rng(0, h), load_x=False, store_out=True)

        K.xbase = h - 1
        ffn_step(K, 0, 0, rng(h + 2, n + 2), load_x=True, store_out=False)
        stash(1, False)
        mixer_step(K, 0, rng(h, n + 2), rng(h + 1, n + 1), False)
        ffn_step(K, 0, 1, rng(h + 1, n + 1), load_x=False, store_out=False)
        ffn_step(K, 1, 0, rng(h + 1, n + 1), load_x=False, store_out=False)
        stash(2, False)
        mixer_step(K, 1, rng(h - 1, n + 1), rng(h, n), False)
        ffn_step(K, 1, 1, rng(h, n), load_x=False, store_out=True)

        S.finalize()
        with nc.Block() as block:
            @block.tensor
            def _(e):
                S.emit("pe", e, esem)

            @block.scalar
            def _(e):
                S.emit("act", e, esem)

            @block.vector
            def _(e):
                S.emit("dve", e, esem)

            @block.gpsimd
            def _(e):
                S.emit("pool", e, esem)

            @block.sync
            def _(e):
                S.emit("sp", e, esem, final_wait=True)
    return nc


def _win_cols():
    cols = []
    cols += list(range(0, 768))
    cols += list(range(768, 1280))
    for g in range(2):
        cols += [1280 + g * 64 + d for d in range(64)] * 2
    cols += list(range(1536, 2048))
    cols += list(range(1408, 1536))
    return np.asarray(cols)


def _wout_rows():
    rows = list(range(0, 256))
    for e in range(4):
        for g in range(2):
            hd = 4 * g + HM[e]
            rows += [256 + hd * 64 + d for d in range(64)]
    rows += list(range(768, 1024))
    return np.asarray(rows)


def _consts():
    cf = np.zeros((128, 256), np.float32)
    cf[:, 0:128] = np.eye(128, dtype=np.float32)
    cf[:, 128:256] = 1.0 / 256.0
    cb = np.zeros((128, 128 + 64 + 512), np.float32)
    cb[:, 0:128] = np.eye(128)
    j = np.arange(128)[:, None]
    i = np.arange(128)[None, :]
    mprev = (i <= j).astype(np.float32)
    mnext = (j <= i).astype(np.float32)
    cb[:, 128:128 + 64] = 1.0
    cb[:, 128 + 64:128 + 64 + 256] = np.tile((mprev - 1.0) * 30000.0, (1, 2))
    cb[:, 128 + 64 + 256:] = np.tile((mnext - 1.0) * 30000.0, (1, 2))
    return cf, cb.astype(ml_dtypes.bfloat16)


def _rope_table(pos):
    half = 32
    inv_freq = (np.float32(ROPE_THETA) ** (-(np.arange(half, dtype=np.float32) / np.float32(half)))).astype(np.float32)
    ang = pos.astype(np.float32)[:, None] * inv_freq[None, :]
    cos = np.cos(ang).astype(np.float32)
    sin = np.sin(ang).astype(np.float32)
    p = np.arange(128)
    d = p % 64
    jj = d % 32
    sign = np.where(d < 32, -1.0, 1.0).astype(np.float32)
    tab = np.empty((128, 2, pos.shape[0]), np.float32)
    tab[:, 0, :] = cos[:, jj].T
    tab[:, 1, :] = sin[:, jj].T * sign[:, None]
    return tab


def _pp(sc_w, cc_w, cc_b, cc_g, cc_be, sink, flip):
    pp = np.zeros((128, DEPTH * NPPL), np.float32)
    p = np.arange(128)
    for l in range(DEPTH):
        scw = sc_w[l][::-1] if flip else sc_w[l]
        ccw = cc_w[l][::-1] if flip else cc_w[l]
        o = l * NPPL
        for c in range(2):
            for k in range(3):
                pp[:, o + c * 3 + k] = scw[k, c * 128 + p]
            for k in range(31):
                pp[:, o + 6 + c * 31 + k] = ccw[k, c * 128 + p]
            pp[:, o + 68 + c] = cc_b[l, c * 128 + p]
            pp[:, o + 70 + c] = cc_g[l, c * 128 + p]
            pp[:, o + 72 + c] = cc_be[l, c * 128 + p]
        for e in range(4):
            pp[:, o + 74 + e] = sink[l, 4 * (p // 64) + HM[e]]
    return pp


_NC_CACHE = {}


def run(inputs, dbg_steps=None, halves=None):
    x = np.asarray(inputs["x"], np.float32)
    B, SEQ, _ = x.shape
    TOK = SEQ // 2
    nb_tok = TOK // 128
    T_in = TOK + 256
    ncores = 2 * B
    key = (nb_tok, dbg_steps, str(halves))
    if key not in _NC_CACHE:
        _NC_CACHE[key] = build(nb_tok, halves=halves, dbg_steps=dbg_steps)
    nc = _NC_CACHE[key]
    f = lambda k: np.ascontiguousarray(np.asarray(inputs[k], np.float32))
    cols = _win_cols()
    rows = _wout_rows()
    win = np.ascontiguousarray(f("w_in")[:, :, cols])
    wout = np.ascontiguousarray(f("w_out")[:, rows, :])
    lng = np.ascontiguousarray(np.stack([f("ln1_g"), f("ln2_g"), f("ln3_g")], axis=1))
    lnb = np.ascontiguousarray(np.stack([f("ln1_b"), f("ln2_b"), f("ln3_b")], axis=1))
    cf, cb = _consts()
    shared = dict(wgu1=f("ffn1_w_gu"), wgu2=f("ffn2_w_gu"), wdn1=f("ffn1_w_down"), wdn2=f("ffn2_w_down"),
                  win=win, wout=wout, lng=lng, lnb=lnb, cstf=cf, cstb=cb)
    in_maps = []
    for c in range(ncores):
        b, hf = c // 2, c % 2
        if hf == 0:
            xs = x[b, 0:T_in]
            pos = np.arange(T_in)
        else:
            xs = x[b, SEQ - T_in:SEQ][::-1]
            pos = SEQ - 1 - np.arange(T_in)
        m = dict(shared)
        m["x"] = np.ascontiguousarray(xs)
        m["rope"] = _rope_table(pos)
        m["pp"] = _pp(f("sc_conv_w"), f("cc_conv_w"), f("cc_conv_b"), f("cc_ln_g"), f("cc_ln_b"), f("attn_sink"),
                      flip=(hf == 1))
        in_maps.append(m)
    res = run_bass_kernel_spmd(nc, in_maps, core_ids=list(range(ncores)))
    out = np.empty((B, SEQ, D), np.float32)
    for c in range(ncores):
        b, hf = c // 2, c % 2
        o = np.asarray(res.results[c]["out"], np.float32)
        if hf == 0:
            out[b, 0:TOK] = o
        else:
            out[b, SEQ - TOK:SEQ] = o[::-1]
    return out


def kernel(**inputs):
    return run(inputs)
```
